# Optimizing a Trainium2 kernel written in Bass

```python
import jax, jax.numpy as jnp
from jax import lax
import numpy as np

D_MODEL = 1024
BATCH = 4
SEQ = 4096
DEPTH = 1

ATTN_HEADS = 8
ATTN_KV_HEADS = 2
ATTN_HEAD_DIM = 64
WINDOW = 128
DN_HEADS = 4
DN_KEY_DIM = 128
DN_VALUE_DIM = 128
CONV_WIDTH = 4
CHUNK = 64
D_FF = 4 * D_MODEL
LN_EPS = 1e-5
RMS_EPS = 1e-6
DEEPNORM_ALPHA = (2 * DEPTH) ** 0.25
DEEPNORM_BETA = (8 * DEPTH) ** -0.25

ATTN_Q_W = ATTN_HEADS * ATTN_HEAD_DIM
ATTN_KV_W = ATTN_KV_HEADS * ATTN_HEAD_DIM
DN_QK_W = DN_HEADS * DN_KEY_DIM
DN_V_W = DN_HEADS * DN_VALUE_DIM
DN_CONV_W = 2 * DN_QK_W + DN_V_W
GATE_W = 2 * D_MODEL
SPLIT_POINTS = (
    ATTN_Q_W,
    ATTN_Q_W + ATTN_KV_W,
    ATTN_Q_W + 2 * ATTN_KV_W,
    ATTN_Q_W + 2 * ATTN_KV_W + DN_CONV_W,
    ATTN_Q_W + 2 * ATTN_KV_W + DN_CONV_W + DN_V_W,
    ATTN_Q_W + 2 * ATTN_KV_W + DN_CONV_W + DN_V_W + DN_HEADS,
    ATTN_Q_W + 2 * ATTN_KV_W + DN_CONV_W + DN_V_W + 2 * DN_HEADS,
)
IN_WIDTH = ATTN_Q_W + 2 * ATTN_KV_W + DN_CONV_W + DN_V_W + 2 * DN_HEADS + GATE_W

kernel_name = "hybrid_swa_sink_gated_deltanet_deepnorm"


def layer_norm(x, g, b):
    xf = x.astype(jnp.float32)
    mu = jnp.mean(xf, axis=-1, keepdims=True)
    var = jnp.mean(jnp.square(xf - mu), axis=-1, keepdims=True)
    return ((xf - mu) * lax.rsqrt(var + LN_EPS) * g + b).astype(x.dtype)


def l2_normalize(x):
    xf = x.astype(jnp.float32)
    return xf * lax.rsqrt(jnp.sum(xf * xf, axis=-1, keepdims=True) + RMS_EPS)


def sliding_window_attention_with_sinks(q, k, v, sinks):
    B, T, Hq, hd = q.shape
    Hkv = k.shape[2]
    G = Hq // Hkv
    n = T // WINDOW
    qb = q.reshape(B, n, WINDOW, Hkv, G, hd)
    kb = k.reshape(B, n, WINDOW, Hkv, hd)
    vb = v.reshape(B, n, WINDOW, Hkv, hd)
    pad = ((0, 0), (1, 0), (0, 0), (0, 0), (0, 0))
    kk = jnp.concatenate([jnp.pad(kb, pad)[:, :-1], kb], axis=2)
    vv = jnp.concatenate([jnp.pad(vb, pad)[:, :-1], vb], axis=2)
    s = jnp.einsum('bnqhgd,bnkhd->bnhgqk', qb, kk).astype(jnp.float32) * (hd ** -0.5)
    qi = jnp.arange(WINDOW)[:, None] + WINDOW
    kj = jnp.arange(2 * WINDOW)[None, :]
    band = (kj <= qi) & (qi - kj < WINDOW)
    has_prev = (jnp.arange(n)[:, None, None] > 0) | (kj[None] >= WINDOW)
    mask = band[None] & has_prev
    s = jnp.where(mask[None, :, None, None], s, -1e30)
    sink = jnp.broadcast_to(sinks.astype(jnp.float32).reshape(1, 1, Hkv, G, 1, 1), s.shape[:-1] + (1,))
    p = jax.nn.softmax(jnp.concatenate([s, sink], axis=-1), axis=-1)[..., :-1]
    o = jnp.einsum('bnhgqk,bnkhd->bnqhgd', p.astype(v.dtype), vv)
    return o.reshape(B, T, Hq * hd)


def causal_depthwise_conv(x, w):
    K, C = w.shape
    return lax.conv_general_dilated(
        x, w[:, None, :].astype(x.dtype), window_strides=(1,), padding=[(K - 1, 0)],
        dimension_numbers=('NWC', 'WIO', 'NWC'), feature_group_count=C)


def gated_delta_rule_chunked(q, k, v, g, beta):
    B, T, H, dk = q.shape
    dv = v.shape[-1]
    n = T // CHUNK

    def chunks(t):
        return t.astype(jnp.float32).reshape(B, n, CHUNK, H, -1).transpose(1, 0, 3, 2, 4)

    q = chunks(q) * (dk ** -0.5)
    k = chunks(k)
    v = chunks(v)
    g = g.astype(jnp.float32).reshape(B, n, CHUNK, H).transpose(1, 0, 3, 2)
    beta = beta.astype(jnp.float32).reshape(B, n, CHUNK, H).transpose(1, 0, 3, 2)
    g = jnp.cumsum(g, axis=-1)
    causal = jnp.tril(jnp.ones((CHUNK, CHUNK), dtype=bool))
    strict = jnp.tril(jnp.ones((CHUNK, CHUNK), dtype=bool), -1)
    decay = jnp.exp(jnp.where(causal, g[..., :, None] - g[..., None, :], -jnp.inf))
    k_beta = k * beta[..., None]
    v_beta = v * beta[..., None]
    a = jnp.where(strict, jnp.einsum('nbhcd,nbhsd->nbhcs', k_beta, k) * decay, 0.0)
    eye = jnp.eye(CHUNK, dtype=jnp.float32)
    t_inv = lax.linalg.triangular_solve(eye + a, jnp.broadcast_to(eye, a.shape),
                                        left_side=True, lower=True, unit_diagonal=True)
    u = jnp.einsum('nbhcs,nbhse->nbhce', t_inv, v_beta)
    w = jnp.einsum('nbhcs,nbhsd->nbhcd', t_inv, k_beta * jnp.exp(g)[..., None])
    attn_intra = jnp.where(causal, jnp.einsum('nbhcd,nbhsd->nbhcs', q, k) * decay, 0.0)

    def step(state, inp):
        q_c, k_c, u_c, w_c, g_c, a_c = inp
        v_new = u_c - jnp.einsum('bhcd,bhde->bhce', w_c, state)
        o_c = (jnp.einsum('bhcd,bhde->bhce', q_c * jnp.exp(g_c)[..., None], state)
               + jnp.einsum('bhcs,bhse->bhce', a_c, v_new))
        g_last = g_c[..., -1]
        k_dec = k_c * jnp.exp(g_last[..., None] - g_c)[..., None]
        state = state * jnp.exp(g_last)[..., None, None] + jnp.einsum('bhcd,bhce->bhde', k_dec, v_new)
        return state, o_c

    state0 = jnp.zeros((B, H, dk, dv), dtype=jnp.float32)
    _, o = lax.scan(step, state0, (q, k, u, w, g, attn_intra))
    return o.transpose(1, 0, 3, 2, 4).reshape(B, T, H, dv)


def hybrid_mixer(x, w_in, conv_w, attn_sinks, dn_a_log, dn_dt_bias, dn_norm_w, w_attn_out, w_dn_out, w_out):
    B, T, _ = x.shape
    proj = x @ w_in
    aq, ak, av, dqkv, dz, db, da, gates = jnp.split(proj, SPLIT_POINTS, axis=-1)
    y_a = sliding_window_attention_with_sinks(
        aq.reshape(B, T, ATTN_HEADS, ATTN_HEAD_DIM),
        ak.reshape(B, T, ATTN_KV_HEADS, ATTN_HEAD_DIM),
        av.reshape(B, T, ATTN_KV_HEADS, ATTN_HEAD_DIM), attn_sinks) @ w_attn_out
    qkv = jax.nn.silu(causal_depthwise_conv(dqkv, conv_w))
    dq, dk_, dv_ = jnp.split(qkv, (DN_QK_W, 2 * DN_QK_W), axis=-1)
    dq = l2_normalize(dq.reshape(B, T, DN_HEADS, DN_KEY_DIM))
    dk_ = l2_normalize(dk_.reshape(B, T, DN_HEADS, DN_KEY_DIM))
    dv_ = dv_.reshape(B, T, DN_HEADS, DN_VALUE_DIM)
    g = -jnp.exp(dn_a_log.astype(jnp.float32)) * jax.nn.softplus(da.astype(jnp.float32) + dn_dt_bias.astype(jnp.float32))
    beta = jax.nn.sigmoid(db.astype(jnp.float32))
    o = gated_delta_rule_chunked(dq, dk_, dv_, g, beta)
    o = o * lax.rsqrt(jnp.mean(o * o, axis=-1, keepdims=True) + RMS_EPS) * dn_norm_w
    o = o * jax.nn.silu(dz.astype(jnp.float32).reshape(B, T, DN_HEADS, DN_VALUE_DIM))
    y_b = o.reshape(B, T, DN_V_W).astype(x.dtype) @ w_dn_out
    g_a, g_b = jnp.split(gates, 2, axis=-1)
    merged = jax.nn.sigmoid(g_a) * y_a + jax.nn.sigmoid(g_b) * y_b
    return merged @ w_out


def setup_inputs(seed: int = 0) -> dict:
    key = jax.random.key(seed)
    ks = jax.random.split(key, 16)
    f32 = jnp.float32
    x = jax.random.normal(ks[0], (BATCH, SEQ, D_MODEL), f32)
    w_in = jax.random.normal(ks[1], (DEPTH, D_MODEL, IN_WIDTH), f32) * D_MODEL ** -0.5
    conv_w = jax.random.normal(ks[2], (DEPTH, CONV_WIDTH, DN_CONV_W), f32) * CONV_WIDTH ** -0.5
    attn_sinks = jax.random.normal(ks[3], (DEPTH, ATTN_HEADS), f32) * 0.5
    dn_a_log = jnp.log(jax.random.uniform(ks[4], (DEPTH, DN_HEADS), f32, 1.0, 16.0))
    dt = jnp.exp(jax.random.uniform(ks[5], (DEPTH, DN_HEADS), f32, float(np.log(1e-3)), float(np.log(1e-1))))
    dn_dt_bias = dt + jnp.log(-jnp.expm1(-dt))
    dn_norm_w = 1.0 + 0.02 * jax.random.normal(ks[6], (DEPTH, DN_VALUE_DIM), f32)
    w_attn_out = jax.random.normal(ks[7], (DEPTH, ATTN_Q_W, D_MODEL), f32) * ATTN_Q_W ** -0.5
    w_dn_out = jax.random.normal(ks[8], (DEPTH, DN_V_W, D_MODEL), f32) * DN_V_W ** -0.5
    w_out = jax.random.normal(ks[9], (DEPTH, D_MODEL, D_MODEL), f32) * (D_MODEL ** -0.5 * DEEPNORM_BETA)
    ln1_g = 1.0 + 0.02 * jax.random.normal(ks[10], (DEPTH, D_MODEL), f32)
    ln1_b = 0.02 * jax.random.normal(ks[11], (DEPTH, D_MODEL), f32)
    w_up = jax.random.normal(ks[12], (DEPTH, D_MODEL, D_FF), f32) * D_MODEL ** -0.5
    w_down = jax.random.normal(ks[13], (DEPTH, D_FF, D_MODEL), f32) * (D_FF ** -0.5 * DEEPNORM_BETA)
    ln2_g = 1.0 + 0.02 * jax.random.normal(ks[14], (DEPTH, D_MODEL), f32)
    ln2_b = 0.02 * jax.random.normal(ks[15], (DEPTH, D_MODEL), f32)
    return {"x": x, "w_in": w_in, "conv_w": conv_w, "attn_sinks": attn_sinks,
            "dn_a_log": dn_a_log, "dn_dt_bias": dn_dt_bias, "dn_norm_w": dn_norm_w,
            "w_attn_out": w_attn_out, "w_dn_out": w_dn_out, "w_out": w_out,
            "ln1_g": ln1_g, "ln1_b": ln1_b, "w_up": w_up, "w_down": w_down,
            "ln2_g": ln2_g, "ln2_b": ln2_b}


def reference(x, w_in, conv_w, attn_sinks, dn_a_log, dn_dt_bias, dn_norm_w, w_attn_out, w_dn_out, w_out,
              ln1_g, ln1_b, w_up, w_down, ln2_g, ln2_b):
    for l in range(DEPTH):
        mix = hybrid_mixer(x, w_in[l], conv_w[l], attn_sinks[l], dn_a_log[l], dn_dt_bias[l], dn_norm_w[l],
                           w_attn_out[l], w_dn_out[l], w_out[l])
        x = layer_norm(DEEPNORM_ALPHA * x + mix, ln1_g[l], ln1_b[l])
        h = jnp.square(jax.nn.relu(x @ w_up[l])) @ w_down[l]
        x = layer_norm(DEEPNORM_ALPHA * x + h, ln2_g[l], ln2_b[l])
    return x
```

```python
import numpy as np
import concourse.bass as bass
import concourse.mybir as mybir
from concourse.bass_utils import run_bass_kernel_spmd

F32 = mybir.dt.float32
BF16 = mybir.dt.bfloat16
AF = mybir.ActivationFunctionType
ALU = mybir.AluOpType

D = 1024
T_CORE = 2048
NT = 16
NEG = -30000.0
ALPHA = 2.0 ** 0.25
LN_EPS = 1e-5
RMS_EPS = 1e-6
IN_WIDTH = 4872
C_AQ, C_AK, C_AV, C_DQKV, C_DZ, C_DB, C_DA, C_G = 0, 512, 640, 768, 2304, 2816, 2820, 2824


class Buf:
    __slots__ = ("name", "w", "r", "excl")

    def __init__(self, name="", excl=False):
        self.name = name
        self.w = None
        self.r = {}
        self.excl = excl

    def _addr(self, tok):
        k = ("eng", tok[1]) if tok[0] == "eng" else ("dma", id(tok[1]))
        old = self.r.get(k)
        if old is None or old[2] < tok[2]:
            self.r[k] = tok


class Lane:
    def __init__(self, sem):
        self.sem = sem
        self.count = 0


class _Op:
    __slots__ = ("emit", "deps", "lane", "signal", "cnt")

    def __init__(self, emit, deps, lane):
        self.emit = emit
        self.deps = deps
        self.lane = lane
        self.signal = False
        self.cnt = 0


class Prog:
    ENGS = ("pe", "act", "dve", "pool", "sp")

    def __init__(self, nc):
        self.nc = nc
        self.ops = {e: [] for e in self.ENGS}
        self.sems = {e: nc.alloc_semaphore("s_" + e) for e in self.ENGS}
        self.lanes = []
        self.pending = {e: [] for e in self.ENGS}

    def lane(self, name):
        l = Lane(self.nc.alloc_semaphore("l_" + name))
        self.lanes.append(l)
        return l

    def barrier(self):
        toks = []
        for e in self.ENGS:
            for i in range(len(self.ops[e]) - 1, -1, -1):
                if self.ops[e][i].lane is None:
                    toks.append(("eng", e, i))
                    break
        for l in self.lanes:
            if l.count > 0:
                toks.append(("dma", l, l.count))
        for e in self.ENGS:
            self.pending[e] = list(toks)

    def op(self, eng, emit, reads=(), writes=(), lane=None, ndma=1):
        idx = len(self.ops[eng])
        deps = {}

        def add(tok):
            if tok is None:
                return
            if tok[0] == "eng":
                if tok[1] == eng and eng in ("pe", "sp"):
                    return
                k = ("eng", tok[1])
            else:
                k = ("dma", id(tok[1]))
            old = deps.get(k)
            if old is None or old[2] < tok[2]:
                deps[k] = tok

        for t in self.pending[eng]:
            add(t)
        self.pending[eng] = []
        for b in reads:
            add(b.w)
            if b.excl:
                for t in b.r.values():
                    if not (t[0] == "eng" and t[1] == eng):
                        add(t)
        for b in writes:
            add(b.w)
            for t in b.r.values():
                add(t)
        if lane is not None:
            lane.count += 16 * ndma
            tok = ("dma", lane, lane.count)
        else:
            tok = ("eng", eng, idx)
        for b in reads:
            b._addr(tok)
        for b in writes:
            b.w = tok
            b.r = {}
        self.ops[eng].append(_Op(emit, list(deps.values()), lane))
        return tok

    def finalize(self, block):
        for e in self.ENGS:
            for o in self.ops[e]:
                for d in o.deps:
                    if d[0] == "eng":
                        self.ops[d[1]][d[2]].signal = True
        for e in self.ENGS:
            c = 0
            for o in self.ops[e]:
                if o.signal:
                    c += 1
                o.cnt = c
        final_lanes = [(l.sem, l.count) for l in self.lanes if l.count > 0]
        final_eng = {e: (self.ops[e][-1] if self.ops[e] else None) for e in self.ENGS}

        def run(ename, eng):
            waited = {}
            for o in self.ops[ename]:
                for d in o.deps:
                    if d[0] == "eng":
                        sem = self.sems[d[1]]
                        val = self.ops[d[1]][d[2]].cnt
                        k = ("e", d[1])
                    else:
                        sem = d[1].sem
                        val = d[2]
                        k = ("l", id(d[1]))
                    if waited.get(k, 0) >= val:
                        continue
                    waited[k] = val
                    eng.wait_ge(sem, val)
                ins = o.emit(eng)
                if o.lane is not None:
                    if not isinstance(ins, (list, tuple)):
                        ins = [ins]
                    for i_ in ins:
                        i_.then_inc(o.lane.sem, 16)
                elif o.signal:
                    ins.then_inc(self.sems[ename], 1)
            if ename == "sp":
                for sem, cnt in final_lanes:
                    eng.wait_ge(sem, cnt)

        block.tensor(lambda e: run("pe", e))
        block.scalar(lambda e: run("act", e))
        block.vector(lambda e: run("dve", e))
        block.gpsimd(lambda e: run("pool", e))
        block.sync(lambda e: run("sp", e))


class Region:
    LO, HI = 16512, 229376
    _n = 0

    def __init__(self, nc, lo, hi):
        assert Region.LO <= lo <= hi <= Region.HI, (lo, hi)
        self.nc, self.lo, self.hi, self.p = nc, lo, hi, lo

    def alloc(self, name, shape, dt):
        esz = 2 if dt == BF16 else 4
        n = esz
        for s in shape[1:]:
            n *= s
        off = (self.p + 63) // 64 * 64
        assert off + n <= self.hi, f"region overflow {name}: need {off + n - self.lo} of {self.hi - self.lo}"
        self.p = off + n
        Region._n += 1
        return self.nc.alloc_sbuf_tensor_at(f"{name}_{Region._n}", list(shape), dt, offset=off)


def build(dbg=False, stop_after=None):
    nc = bass.Bass("TRN2", target_bir_lowering=False)
    P = Prog(nc)
    KB = 1024
    BASE = Region.LO

    def din(name, shape):
        return nc.dram_tensor(name, list(shape), F32, kind="ExternalInput").ap()

    x_d = din("x", [T_CORE, D])
    xp_d = din("xp", [T_CORE, D])
    mprev0_d = din("mprev0", [128, 512])
    cst_d = din("cst", [128, 3584])
    cw_d = din("cw", [128, 48])
    small_d = din("small", [128, 16])
    nw_d = din("nw", [128, 128])
    lnp_d = din("lnp", [128, 4 * D])
    w_in_d = din("w_in", [D, IN_WIDTH])
    w_ao_d = din("w_ao", [512, D])
    w_do_d = din("w_do", [512, D])
    w_out_d = din("w_out", [D, D])
    w_up_d = din("w_up", [D, 4 * D])
    w_down_d = din("w_down", [4 * D, D])
    out_d = nc.dram_tensor("out", [T_CORE, D], F32, kind="ExternalOutput").ap()
    dbg_out = {}

    def ddump(name, shape, dt=F32):
        dbg_out[name] = nc.dram_tensor(name, list(shape), dt, kind="ExternalOutput").ap()
        return dbg_out[name]

    w_in_v = w_in_d.rearrange("(kc p) c -> p kc c", p=128)

    PS = [nc.alloc_psum_tensor(f"ps{i}", [128, 1024], F32) for i in range(4)]

    def bank(i):
        return PS[i // 2][:, (i % 2) * 512:(i % 2 + 1) * 512]

    def bank_bf(i):
        return bank(i).bitcast(BF16)

    RG = Region(nc, BASE, BASE + 12 * KB)
    ident_f = RG.alloc("ident_f", [128, 128], F32)
    U_f = RG.alloc("U_f", [128, 128], F32)
    ones_f = RG.alloc("ones_f", [128, 128], F32)
    MD_f = RG.alloc("MD_f", [128, 128], F32)
    sel_f = RG.alloc("sel_f", [128, 4, 128], F32)
    ident_b = RG.alloc("ident_b", [128, 128], BF16)
    ones_b = RG.alloc("ones_b", [128, 128], BF16)
    Mown_b = RG.alloc("Mown_b", [128, 512], BF16)
    Mprev_b = RG.alloc("Mprev_b", [128, 512], BF16)
    Mprev0_b = RG.alloc("Mprev0_b", [128, 512], BF16)
    cw_s = RG.alloc("cw_s", [128, 12, 4], F32)
    small_s = RG.alloc("small_s", [128, 16], F32)
    nega_s = RG.alloc("nega_s", [128, 4], F32)
    esink_s = RG.alloc("esink_s", [128, 8], F32)
    nw_s = RG.alloc("nw_s", [128, 128], F32)
    B_cst = Buf("cst")
    L_cst = P.lane("cst")

    def ld_consts(e):
        return [
            e.dma_start(out=ident_f[:], in_=cst_d[:, 0:128]),
            e.dma_start(out=U_f[:], in_=cst_d[:, 128:256]),
            e.dma_start(out=ones_f[:], in_=cst_d[:, 256:384]),
            e.dma_start(out=MD_f[:], in_=cst_d[:, 384:512]),
            e.dma_start(out=sel_f[:], in_=cst_d[:, 1536:2048].rearrange("p (h f) -> p h f", h=4)),
            e.dma_start(out=cw_s[:], in_=cw_d.rearrange("p (c t) -> p c t", c=12)),
            e.dma_start(out=small_s[:], in_=small_d),
            e.dma_start(out=nw_s[:], in_=nw_d),
        ]
    P.op("sp", ld_consts, writes=[B_cst], lane=L_cst, ndma=8)
    B_cstb = Buf("cstb")
    L_cstb = P.lane("cstb")

    def ld_consts_b(e):
        return [
            e.dma_start(out=ident_b[:], in_=cst_d[:, 0:128]),
            e.dma_start(out=ones_b[:], in_=cst_d[:, 256:384]),
            e.dma_start(out=Mown_b[:], in_=cst_d[:, 512:1024]),
            e.dma_start(out=Mprev_b[:], in_=cst_d[:, 1024:1536]),
            e.dma_start(out=Mprev0_b[:], in_=mprev0_d),
        ]
    P.op("pool", ld_consts_b, writes=[B_cstb], lane=L_cstb, ndma=5)
    B_der = Buf("derived")
    P.op("act", lambda e: e.activation(out=nega_s[:], in_=small_s[:, 8:12], func=AF.Exp), reads=[B_cst], writes=[B_der])
    P.op("act", lambda e: e.activation(out=esink_s[:], in_=small_s[:, 0:8], func=AF.Exp), reads=[B_cst], writes=[B_der])
    P.op("dve", lambda e: e.tensor_scalar(out=nega_s[:], in0=nega_s[:], scalar1=-1.0, scalar2=None, op0=ALU.mult), reads=[B_der], writes=[B_der])

    R_OB = (BASE + 12 * KB, BASE + 28 * KB)
    R_OA = (BASE + 28 * KB, BASE + 44 * KB)
    R_MG = (BASE + 44 * KB, BASE + 76 * KB)
    R_ACC = (BASE + 76 * KB, BASE + 140 * KB)
    R_X1T = (BASE + 140 * KB, BASE + 172 * KB)
    TOP = Region.HI
    obT = Region(nc, *R_OB).alloc("obT", [128, 4, T_CORE], BF16)
    oaT = Region(nc, *R_OA).alloc("oaT", [128, 4, T_CORE], BF16)
    mgT = Region(nc, *R_MG).alloc("mgT", [128, 8, T_CORE], BF16)
    acc = Region(nc, *R_ACC).alloc("acc", [128, NT, D], F32)
    x1T = Region(nc, *R_X1T).alloc("x1T", [128, 8, T_CORE], BF16)
    B_obT = [Buf(f"obT{t}") for t in range(NT)]
    B_oaT = [Buf(f"oaT{t}") for t in range(NT)]
    B_mgT = [Buf(f"mgT{s}") for s in range(4)]
    B_acc = [Buf(f"acc{t}") for t in range(NT)]
    B_x1T = [Buf(f"x1T{t}") for t in range(NT)]

    def load_w(dst, dst_buf, lane, src_cols, eng="pool"):
        c0, c1 = src_cols

        def emit(e):
            return [e.dma_start(out=dst[:, 2 * j:2 * j + 2, :], in_=w_in_v[:, 2 * j:2 * j + 2, c0:c1]) for j in range(4)]
        P.op(eng, emit, writes=[dst_buf], lane=lane, ndma=4)

    class XT:
        _cnt = [0]

        def __init__(self, R, psbank, B_ps, nslots=2):
            XT._cnt[0] += 1
            self.xb = [R.alloc(f"xb{i}", [128, D], BF16) for i in range(nslots)]
            self.B_xb = [Buf(f"xb{i}") for i in range(nslots)]
            self.L_xb = [P.lane(f"xb{i}_{XT._cnt[0]}") for i in range(nslots)]
            self.n = 0
            self.psbank = psbank
            self.B_ps = B_ps
            self.nslots = nslots

        def tile(self, src_rows, dst, dst_cols, dst_buf, evac="act"):
            s = self.n % self.nslots
            self.n += 1
            xb, Bx, Lx = self.xb[s], self.B_xb[s], self.L_xb[s]
            P.op("pool", lambda e: e.dma_start(out=xb[:], in_=src_rows), writes=[Bx], lane=Lx)
            pst = bank_bf(self.psbank)
            for kc in range(8):
                P.op("pe", lambda e, kc=kc: e.transpose(pst[:, kc * 128:(kc + 1) * 128], xb[:, kc * 128:(kc + 1) * 128], ident_b[:]),
                     reads=[Bx, B_cstb], writes=[self.B_ps])
            c0, c1 = dst_cols
            src = pst.rearrange("p (k t) -> p k t", k=8)
            if evac == "act":
                P.op("act", lambda e: e.copy(out=dst[:, :, c0:c1], in_=src), reads=[self.B_ps], writes=[dst_buf])
            else:
                P.op("dve", lambda e: e.tensor_copy(out=dst[:, :, c0:c1], in_=src), reads=[self.B_ps], writes=[dst_buf])

    def proj_fm(ps_ap, ps_buf, W, wbuf, wc0, xT, xbuf, x0, n):
        for kc in range(8):
            P.op("pe", lambda e, kc=kc: e.matmul(ps_ap, lhsT=W[:, kc, wc0:wc0 + 128], rhs=xT[:, kc, x0:x0 + n],
                                                 start=(kc == 0), stop=(kc == 7)),
                 reads=[wbuf, xbuf], writes=[ps_buf])

    def proj_tm(ps_ap, ps_buf, W, wbuf, wc0, ncols, xT, xbuf, x0):
        for kc in range(8):
            P.op("pe", lambda e, kc=kc: e.matmul(ps_ap, lhsT=xT[:, kc, x0:x0 + 128], rhs=W[:, kc, wc0:wc0 + ncols],
                                                 start=(kc == 0), stop=(kc == 7)),
                 reads=[wbuf, xbuf], writes=[ps_buf])

    def run_chains(gens):
        active = list(gens)
        while active:
            for g_ in list(active):
                try:
                    next(g_)
                except StopIteration:
                    active.remove(g_)

    def run_window(gens, width, stagger=1):
        gens = list(gens)
        active = []
        since = stagger
        while gens or active:
            if gens and len(active) < width and since >= stagger:
                active.append(gens.pop(0))
                since = 0
            since += 1
            for g_ in list(active):
                try:
                    next(g_)
                except StopIteration:
                    active.remove(g_)

    def phase_1a():
        R = Region(nc, BASE + 28 * KB, TOP)
        Wd = R.alloc("Wd", [128, 8, 1536], BF16)
        Wba = R.alloc("Wba", [128, 8, 8], BF16)
        Wz = R.alloc("Wz", [128, 8, 512], BF16)
        B_Wd = [Buf("Wdq"), Buf("Wdk"), Buf("Wdv")]
        B_Wba, B_Wz = Buf("Wba"), Buf("Wz")
        lw = [P.lane(f"w1a{i}") for i in range(7)]

        def load_wd(i):
            def emit(e, i=i):
                return [e.dma_start(out=Wd[:, 2 * j:2 * j + 2, i * 512:(i + 1) * 512],
                                    in_=w_in_v[:, 2 * j:2 * j + 2, C_DQKV + i * 512:C_DQKV + (i + 1) * 512]) for j in range(4)]
            P.op("pool", emit, writes=[B_Wd[i]], lane=lw[i], ndma=4)

        MD4_b = R.alloc("MD4_b", [128, 512], BF16)
        nsb_f = R.alloc("nsb_f", [128, 1024], F32)
        B_c1a, B_c1b = Buf("c1a"), Buf("c1b")
        P.op("sp", lambda e: e.dma_start(out=nsb_f[:], in_=cst_d[:, 2560:3584]), writes=[B_c1b], lane=lw[6])

        def early_loads():
            P.op("pool", lambda e: e.dma_start(out=Wba[:], in_=w_in_v[:, :, C_DB:C_DB + 8]), writes=[B_Wba], lane=lw[3])
            load_wd(1)

        def late_loads():
            load_wd(0)
            load_wd(2)
            P.op("pool", lambda e: e.dma_start(out=MD4_b[:], in_=cst_d[:, 2048:2560]), writes=[B_c1a], lane=lw[5])

            def emit_z(e):
                return [e.dma_start(out=Wz[:, 2 * j:2 * j + 2, :], in_=w_in_v[:, 2 * j:2 * j + 2, C_DZ:C_DZ + 512]) for j in range(4)]
            P.op("pool", emit_z, writes=[B_Wz], lane=lw[4], ndma=4)
        first_stage = [True]
        negsel = nsb_f[0:4, 0:512]
        blk = nsb_f[0:4, 512:1024]

        B_bk = [Buf(f"bank{i}", excl=True) for i in range(8)]
        xt = XT(R, 0, B_bk[0])
        xT1 = R.alloc("xT", [128, 8, 512], BF16)
        xT = [xT1, xT1]
        B_xT1 = Buf("xT")
        B_xT = [B_xT1, B_xT1]
        NR = 2
        raw = [R.alloc(f"raw{i}", [128, 515], F32) for i in range(NR)]
        B_raw = [Buf(f"raw{i}") for i in range(NR)]
        cacc = [R.alloc(f"cacc{i}", [128, 512], F32) for i in range(NR)]
        B_cacc = [Buf(f"cacc{i}") for i in range(NR)]
        carry = R.alloc("carry", [128, 12, 3], F32)
        B_carry = [Buf(f"carry{i}") for i in range(12)]
        s_qk = R.alloc("s_qk", [128, 8, 512], F32)
        B_sqk = [Buf(f"sqk{i}") for i in range(8)]
        vT = R.alloc("vT", [128, 4, 512], BF16)
        B_vT = [Buf(f"vT{i}") for i in range(4)]
        qT2 = [R.alloc(f"qT_{i}", [128, 4, 512], BF16) for i in range(2)]
        kT2 = [R.alloc(f"kT_{i}", [128, 4, 512], BF16) for i in range(2)]
        B_qT2 = [[Buf(f"qT{j}{i}") for i in range(4)] for j in range(2)]
        B_kT2 = [[Buf(f"kT{j}{i}") for i in range(4)] for j in range(2)]
        sq_b = [R.alloc(f"sq_b{i}", [128, 512], BF16) for i in range(2)]
        B_sqb = [Buf(f"sqb{i}") for i in range(2)]
        rs = [R.alloc(f"rs{i}", [128, 512], F32) for i in range(2)]
        B_rs = [Buf(f"rs{i}") for i in range(2)]
        k_tm2 = [R.alloc(f"k_tm{i}", [128, 4, 512], BF16) for i in range(2)]
        v_tm2 = [R.alloc(f"v_tm{i}", [128, 4, 512], BF16) for i in range(2)]
        B_ktm2 = [[Buf(f"ktm{j}{i}") for i in range(4)] for j in range(2)]
        B_vtm2 = [[Buf(f"vtm{j}{i}") for i in range(4)] for j in range(2)]
        ba = R.alloc("ba", [128, 4, 8], F32)
        beta2 = [R.alloc(f"beta{i}", [128, 4, 4], F32) for i in range(2)]
        gg2 = [R.alloc(f"gg{i}", [128, 4, 4], F32) for i in range(2)]
        gtmp = R.alloc("gtmp", [128, 4, 4], F32)
        B_ba = Buf("ba")
        B_beta2 = [Buf("beta0"), Buf("beta1")]
        nbeta2 = [R.alloc(f"nbeta{i}", [128, 4, 4], F32) for i in range(2)]
        B_gg2 = [Buf("gg0"), Buf("gg1")]
        B_gt = Buf("gtmp")
        G2 = R.alloc("G2", [128, 4, 512], F32)
        B_G2 = [Buf(f"G2{i}") for i in range(4)]
        zs, B_zs = rs, B_rs

        def two(name, shape, dt):
            return [R.alloc(f"{name}{i}", shape, dt) for i in range(2)], [Buf(f"{name}{i}") for i in range(2)]

        def one(name, shape, dt):
            return R.alloc(name, shape, dt), Buf(name)
        gcl4 = [R.alloc(f"gcl4_{i}", [128, 4, 12], F32) for i in range(2)]
        eg4 = [R.alloc(f"eg4_{i}", [128, 4, 12], F32) for i in range(2)]
        B_gcl4 = [Buf("gcl4_0"), Buf("gcl4_1")]
        B_eg4 = [Buf("eg4_0"), Buf("eg4_1")]
        gcT, B_gcT = two("gcT", [128, 128], F32)
        gcTb, B_gcTb = two("gcTb", [128, 512], F32)
        DT, B_DT = two("DT", [128, 512], F32)
        DE, B_DE = two("DE", [128, 512], F32)
        N0, B_N0 = two("N0", [128, 512], BF16)
        N0T, B_N0T = two("N0T", [128, 512], BF16)
        Na, B_Na = two("Na", [128, 512], BF16)
        NaT, B_NaT = two("NaT", [128, 512], BF16)
        Nb, B_Nb = two("Nb", [128, 512], BF16)
        NbT, B_NbT = two("NbT", [128, 512], BF16)
        Pm, B_Pm = two("Pm", [128, 512], BF16)
        kg, B_kg = two("kg", [128, 512], BF16)
        AI, B_AI = two("AI", [128, 512], BF16)
        qg, B_qg = two("qg", [128, 512], BF16)
        kdec, B_kdec = two("kdec", [128, 512], BF16)
        wT, B_wT = two("wT", [128, 512], BF16)
        ub, B_ub = two("ub", [128, 512], F32)
        vtmp, B_vtmp = one("vtmp", [128, 512], F32)
        vnew, B_vnew = one("vnew", [128, 512], BF16)
        S_f, B_Sf = one("S_f", [128, 512], F32)
        S_d, B_Sd = one("S_d", [128, 512], F32)
        S_b, B_Sb = one("S_b", [128, 512], BF16)
        osq, B_osq = vtmp, B_vtmp
        ssq, B_ssq = one("ssq", [128, 4], F32)
        rstd, B_rstd = one("rstd", [128, 4], F32)
        G2r, B_G2r = vtmp, B_vtmp
        og, B_og = two("og", [128, 512], BF16)

        PJ_BANK = (1, 0)
        PB = ((2, 3), (4, 5))
        BR, BO = 6, 7

        def v4(ap):
            return ap.rearrange("p (h f) -> p h f", h=4)

        def bc4(col_ap):
            return col_ap.unsqueeze(2).to_broadcast([128, 4, 128])

        P.op("pool", lambda e: e.memset(S_f[:], 0.0), writes=[B_Sf])
        P.op("pool", lambda e: e.memset(S_b[:], 0.0), writes=[B_Sb])
        P.op("pool", lambda e: e.memset(carry[:], 0.0), writes=B_carry)

        def chunk_chain(ci, xTs, BxT, slot):
            part, h = ci // 4, ci % 4
            pj, Bpj = bank(PJ_BANK[slot]), B_bk[PJ_BANK[slot]]
            for kc in range(8):
                P.op("pe", lambda e, kc=kc: e.matmul(pj, lhsT=Wd[:, kc, ci * 128:ci * 128 + 128], rhs=xTs[:, kc, 0:512], start=(kc == 0), stop=(kc == 7)),
                     reads=[B_Wd[part], BxT], writes=[Bpj])
                if kc % 2 == 1 and kc < 7:
                    yield
            rw, Brw = raw[slot], B_raw[slot]
            ca, Bca = cacc[slot], B_cacc[slot]
            P.op("pool", lambda e: e.tensor_copy(out=rw[:, 0:3], in_=carry[:, ci, :]), reads=[B_carry[ci]], writes=[Brw])
            yield
            P.op("act", lambda e: e.copy(out=rw[:, 3:515], in_=pj), reads=[Bpj], writes=[Brw])
            P.op("act", lambda e: e.activation(out=ca[:], in_=pj, func=AF.Identity, scale=cw_s[:, ci, 3:4]), reads=[Bpj, B_cst], writes=[Bca])
            yield
            P.op("pool", lambda e: e.tensor_copy(out=carry[:, ci, :], in_=rw[:, 512:515]), reads=[Brw], writes=[B_carry[ci]])
            for tap in range(3):
                P.op("dve", lambda e, tap=tap: e.scalar_tensor_tensor(
                    out=ca[:], in0=rw[:, tap:tap + 512], scalar=cw_s[:, ci, tap:tap + 1], in1=ca[:], op0=ALU.mult, op1=ALU.add),
                    reads=[Brw, Bca, B_cst], writes=[Bca])
                yield
            if part < 2:
                P.op("act", lambda e: e.activation(out=s_qk[:, ci, :], in_=ca[:], func=AF.Silu), reads=[Bca], writes=[B_sqk[ci]])
            else:
                P.op("act", lambda e: e.activation(out=vT[:, h, :], in_=ca[:], func=AF.Silu), reads=[Bca], writes=[B_vT[h]])
            yield

        def norm_chain(qi, sp):
            qT, kT, B_qT, B_kT = qT2[sp], kT2[sp], B_qT2[sp], B_kT2[sp]
            part, h = qi // 4, qi % 4
            sb_, Bsb = sq_b[qi % 2], B_sqb[qi % 2]
            r_, Br = rs[qi % 2], B_rs[qi % 2]
            bk_ = PJ_BANK[qi % 2]
            P.op("act", lambda e: e.activation(out=sb_[:], in_=s_qk[:, qi, :], func=AF.Square), reads=[B_sqk[qi]], writes=[Bsb])
            yield
            P.op("pe", lambda e: e.matmul(bank(bk_), lhsT=ones_b[:], rhs=sb_[:], start=True, stop=True), reads=[Bsb, B_cstb], writes=[B_bk[bk_]])
            yield
            P.op("act", lambda e: e.activation(out=r_[:], in_=bank(bk_), func=AF.Ln, bias=RMS_EPS), reads=[B_bk[bk_]], writes=[Br])
            P.op("act", lambda e: e.activation(out=r_[:], in_=r_[:], func=AF.Exp, scale=-0.5), reads=[Br], writes=[Br])
            yield
            if part == 0:
                P.op("dve", lambda e: e.scalar_tensor_tensor(out=qT[:, h, :], in0=s_qk[:, qi, :], scalar=128.0 ** -0.5, in1=r_[:],
                                                             op0=ALU.mult, op1=ALU.mult), reads=[B_sqk[qi], Br], writes=[B_qT[h]])
            else:
                P.op("dve", lambda e: e.tensor_tensor(out=kT[:, h, :], in0=s_qk[:, qi, :], in1=r_[:], op=ALU.mult),
                     reads=[B_sqk[qi], Br], writes=[B_kT[h]])
            yield

        def pre_chain(t, own, c, sp, can_tail):
            par = c
            qT, kT, B_qT, B_kT = qT2[sp], kT2[sp], B_qT2[sp], B_kT2[sp]
            k_tm, v_tm, B_ktm, B_vtm = k_tm2[sp], v_tm2[sp], B_ktm2[sp], B_vtm2[sp]
            beta, B_beta = beta2[sp], B_beta2[sp]
            tc0 = t * 128
            gl_, Bgl = gcl4[sp][:, t, :], B_gcl4[sp]
            eg_, Beg = eg4[sp][:, t, :], B_eg4[sp]
            bx, by = PB[c]
            X_, Y_, BX, BY = bank(bx), bank(by), B_bk[bx], B_bk[by]
            gcT_, BgT, gcTb_, BgTb = gcT[c], B_gcT[c], gcTb[c], B_gcTb[c]
            DT_, BDT, DE_, BDE = DT[c], B_DT[c], DE[c], B_DE[c]
            N0_, BN0, N0T_, BN0T, Pm_, BPm, kg_, Bkg = N0[c], B_N0[c], N0T[c], B_N0T[c], Pm[c], B_Pm[c], kg[c], B_kg[c]
            P.op("pe", lambda e: e.transpose(Y_[0:4, 0:128], gl_[:, 0:4], ident_f[:]), reads=[Bgl, B_cst], writes=[BY])
            for h in range(4):
                kTh = kT[:, h, tc0:tc0 + 128]
                P.op("pe", lambda e, kTh=kTh, h=h: e.matmul(X_[:, h * 128:(h + 1) * 128], lhsT=kTh, rhs=kTh, start=True, stop=True), reads=[B_kT[h]], writes=[BX])
            yield
            P.op("act", lambda e: e.copy(out=gcT_[0:4, :], in_=Y_[0:4, 0:128]), reads=[BY], writes=[BgT])
            P.op("dve", lambda e: e.tensor_tensor(out=gcTb_[0:4, :].rearrange("p (h f) -> p h f", h=4), in0=Y_[0:4, 0:128].unsqueeze(1).to_broadcast([4, 4, 128]),
                                                  in1=blk.rearrange("p (h f) -> p h f", h=4), op=ALU.mult), reads=[BY, B_c1b], writes=[BgTb])
            yield
            P.op("pe", lambda e: e.matmul(Y_, lhsT=ones_f[0:4, :], rhs=gcTb_[0:4, :], start=True, stop=False), reads=[BgTb, B_cst], writes=[BY])
            P.op("pe", lambda e: e.matmul(Y_, lhsT=gcT_[0:4, :], rhs=negsel, start=False, stop=False), reads=[BgT, B_c1b], writes=[BY])
            P.op("pe", lambda e: e.matmul(Y_, lhsT=ident_b[:], rhs=MD4_b[:], start=False, stop=True), reads=[B_cstb, B_c1a], writes=[BY])
            yield
            P.op("act", lambda e: e.activation(out=DT_[:], in_=Y_, func=AF.Exp), reads=[BY], writes=[BDT])
            yield
            for h in range(4):
                hs = slice(h * 128, (h + 1) * 128)
                P.op("dve", lambda e, hs=hs, h=h: e.scalar_tensor_tensor(out=N0_[:, hs], in0=X_[:, hs], scalar=beta[:, t, h:h + 1], in1=DT_[:, hs], op0=ALU.mult, op1=ALU.mult),
                     reads=[BX, B_beta, BDT], writes=[BN0])
            yield
            pstb = Y_.bitcast(BF16)
            for h in range(4):
                P.op("pe", lambda e, h=h: e.transpose(pstb[:, h * 128:(h + 1) * 128], N0_[:, h * 128:(h + 1) * 128], ident_b[:]), reads=[BN0, B_cstb], writes=[BY])
            P.op("dve", lambda e: e.tensor_tensor(out=v4(Pm_[:]), in0=ident_b[:].unsqueeze(1).to_broadcast([128, 4, 128]), in1=v4(N0_[:]), op=ALU.subtract), reads=[BN0, B_cstb], writes=[BPm])
            P.op("dve", lambda e: e.tensor_tensor(out=v4(kg_[:]), in0=v4(k_tm[:, t, :]), in1=bc4(eg_[:, 0:4]), op=ALU.mult), reads=[B_ktm[t], Beg], writes=[Bkg])
            yield
            P.op("act", lambda e: e.copy(out=N0T_[:], in_=pstb[:, 0:512]), reads=[BY], writes=[BN0T])
            yield
            cur, curT, Bc, BcT = N0_, N0T_, BN0, BN0T
            pp = ((Na[c], NaT[c], B_Na[c], B_NaT[c]), (Nb[c], NbT[c], B_Nb[c], B_NbT[c]))
            tb, mb, Btb, Bmb = Y_, X_, BY, BX
            for lv in range(1, 7):
                nxt, nxtT, Bn, BnT = pp[lv % 2]
                for h in range(4):
                    hs = slice(h * 128, (h + 1) * 128)
                    P.op("pe", lambda e, cur=cur, curT=curT, hs=hs, tb=tb: e.matmul(tb[:, hs], lhsT=cur[:, hs], rhs=curT[:, hs], start=True, stop=True), reads=[Bc, BcT], writes=[Btb])
                if lv < 6:
                    for h in range(4):
                        hs = slice(h * 128, (h + 1) * 128)
                        P.op("pe", lambda e, cur=cur, curT=curT, hs=hs, mb=mb: e.matmul(mb[:, hs], lhsT=curT[:, hs], rhs=cur[:, hs], start=True, stop=True), reads=[Bc, BcT], writes=[Bmb])
                yield
                P.op("act", lambda e, nxtT=nxtT, tb=tb: e.copy(out=nxtT[:], in_=tb), reads=[Btb], writes=[BnT])
                if lv < 6:
                    P.op("dve", lambda e, nxt=nxt, mb=mb: e.tensor_copy(out=nxt[:], in_=mb), reads=[Bmb], writes=[Bn])
                yield
                for h in range(4):
                    hs = slice(h * 128, (h + 1) * 128)
                    P.op("pe", lambda e, nxtT=nxtT, hs=hs, tb=tb: e.matmul(tb[:, hs], lhsT=nxtT[:, hs], rhs=Pm_[:, hs], start=True, stop=True), reads=[BnT, BPm], writes=[Btb])
                yield
                P.op("dve", lambda e, tb=tb: e.tensor_tensor(out=Pm_[:], in0=tb, in1=Pm_[:], op=ALU.add), reads=[Btb, BPm], writes=[BPm])
                yield
                cur, curT, Bc, BcT = nxt, nxtT, Bn, BnT
                tb, mb, Btb, Bmb = mb, tb, Bmb, Btb
            while not can_tail():
                yield
            for h in range(4):
                hs = slice(h * 128, (h + 1) * 128)
                P.op("pe", lambda e, hs=hs, tb=tb: e.matmul(tb[:, hs], lhsT=Pm_[:, hs], rhs=v_tm[:, t, hs], start=True, stop=True), reads=[BPm, B_vtm[t]], writes=[Btb])
            for h in range(4):
                hs = slice(h * 128, (h + 1) * 128)
                P.op("pe", lambda e, hs=hs, mb=mb: e.matmul(mb[:, hs], lhsT=kg_[:, hs], rhs=Pm_[:, hs], start=True, stop=True), reads=[Bkg, BPm], writes=[Bmb])
            P.op("dve", lambda e: e.tensor_tensor(out=v4(kdec[par][:]), in0=v4(k_tm[:, t, :]), in1=bc4(eg_[:, 4:8]), op=ALU.mult), reads=[B_ktm[t], Beg], writes=[B_kdec[par]])
            yield
            P.op("dve", lambda e, tb=tb: e.tensor_tensor(out=v4(ub[par][:]), in0=v4(tb), in1=bc4(beta[:, t, :]), op=ALU.mult), reads=[Btb, B_beta], writes=[B_ub[par]])
            P.op("act", lambda e, mb=mb: e.copy(out=wT[par][:], in_=mb), reads=[Bmb], writes=[B_wT[par]])
            yield
            if own:
                for h in range(4):
                    kTh = kT[:, h, tc0:tc0 + 128]
                    qTh = qT[:, h, tc0:tc0 + 128]
                    P.op("pe", lambda e, kTh=kTh, qTh=qTh, h=h, tb=tb: e.matmul(tb[:, h * 128:(h + 1) * 128], lhsT=kTh, rhs=qTh, start=True, stop=True), reads=[B_kT[h], B_qT[h]], writes=[Btb])
                P.op("pe", lambda e, mb=mb: e.matmul(mb, lhsT=ones_f[0:4, :], rhs=gcTb_[0:4, :], start=True, stop=True), reads=[BgTb, B_cst], writes=[Bmb])
                P.op("dve", lambda e: e.tensor_tensor(out=v4(DT_[:]), in0=v4(DT_[:]), in1=ident_f[:].unsqueeze(1).to_broadcast([128, 4, 128]), op=ALU.add),
                     reads=[BDT, B_cst], writes=[BDT])
                yield
                P.op("dve", lambda e, tb=tb: e.tensor_tensor(out=AI[par][:], in0=tb, in1=DT_[:], op=ALU.mult), reads=[Btb, BDT], writes=[B_AI[par]])
                P.op("act", lambda e, mb=mb: e.activation(out=DE_[:], in_=mb, func=AF.Exp), reads=[Bmb], writes=[BDE])
                yield
                P.op("dve", lambda e: e.tensor_tensor(out=v4(qg[par][:]), in0=qT[:, :, tc0:tc0 + 128], in1=v4(DE_[:]), op=ALU.mult), reads=B_qT + [BDE], writes=[B_qg[par]])
                yield

        def rec_chain(t, own, par, gtile, sp):
            beta, B_beta = beta2[sp], B_beta2[sp]
            R_, O_ = bank(BR), bank(BO)
            BBR, BBO = B_bk[BR], B_bk[BO]
            eg_, Beg = eg4[sp][:, t, :], B_eg4[sp]
            for h in range(4):
                hs = slice(h * 128, (h + 1) * 128)
                P.op("pe", lambda e, hs=hs: e.matmul(R_[:, hs], lhsT=wT[par][:, hs], rhs=S_b[:, hs], start=True, stop=True), reads=[B_wT[par], B_Sb], writes=[BBR])
            if own:
                for h in range(4):
                    hs = slice(h * 128, (h + 1) * 128)
                    P.op("pe", lambda e, hs=hs, h=h: e.matmul(O_[:, hs], lhsT=qg[par][:, hs], rhs=S_b[:, hs], start=(h == 0), stop=False, skip_group_check=True), reads=[B_qg[par], B_Sb], writes=[BBO])
            yield
            for h in range(4):
                hs = slice(h * 128, (h + 1) * 128)
                P.op("dve", lambda e, hs=hs, h=h: e.scalar_tensor_tensor(out=vnew[:, hs], in0=R_[:, hs], scalar=nbeta2[sp][:, t, h:h + 1], in1=ub[par][:, hs], op0=ALU.mult, op1=ALU.add),
                     reads=[BBR, B_beta, B_ub[par]], writes=[B_vnew])
            yield
            for h in range(4):
                hs = slice(h * 128, (h + 1) * 128)
                P.op("pe", lambda e, hs=hs: e.matmul(R_[:, hs], lhsT=kdec[par][:, hs], rhs=vnew[:, hs], start=True, stop=True), reads=[B_kdec[par], B_vnew], writes=[BBR])
            yield
            for h in range(4):
                hs = slice(h * 128, (h + 1) * 128)
                P.op("dve", lambda e, hs=hs, h=h: e.scalar_tensor_tensor(out=S_f[:, hs], in0=S_f[:, hs], scalar=eg_[:, 8 + h:9 + h], in1=R_[:, hs], op0=ALU.mult, op1=ALU.add),
                     reads=[BBR, Beg, B_Sf], writes=[B_Sf])
            if own:
                for h in range(4):
                    hs = slice(h * 128, (h + 1) * 128)
                    P.op("pe", lambda e, hs=hs, h=h: e.matmul(O_[:, hs], lhsT=AI[par][:, hs], rhs=vnew[:, hs], start=False, stop=(h == 3), skip_group_check=True), reads=[B_AI[par], B_vnew], writes=[BBO])
            yield
            P.op("act", lambda e: e.copy(out=S_b[:], in_=S_f[:]), reads=[B_Sf], writes=[B_Sb])
            if not own:
                return
            yield
            P.op("act", lambda e: e.activation(out=osq[:], in_=O_, func=AF.Square), reads=[BBO], writes=[B_osq])
            yield
            P.op("dve", lambda e: e.tensor_reduce(out=ssq[:], in_=v4(osq[:]), axis=mybir.AxisListType.X, op=ALU.add), reads=[B_osq], writes=[B_ssq])
            P.op("dve", lambda e: e.tensor_scalar(out=rstd[:], in0=ssq[:], scalar1=1.0 / 128.0, scalar2=RMS_EPS, op0=ALU.mult, op1=ALU.add), reads=[B_ssq], writes=[B_rstd])
            yield
            P.op("act", lambda e: e.activation(out=rstd[:], in_=rstd[:], func=AF.Ln), reads=[B_rstd], writes=[B_rstd])
            P.op("act", lambda e: e.activation(out=rstd[:], in_=rstd[:], func=AF.Exp, scale=-0.5), reads=[B_rstd], writes=[B_rstd])
            yield
            P.op("dve", lambda e: e.tensor_tensor(out=v4(G2r[:]), in0=v4(G2[:, t, :]), in1=bc4(rstd[:]), op=ALU.mult), reads=[B_G2[t], B_rstd], writes=[B_G2r])
            yield
            o_, Bo = og[t % 2], B_og[t % 2]
            P.op("dve", lambda e: e.tensor_tensor(out=o_[:], in0=O_, in1=G2r[:], op=ALU.mult), reads=[BBO, B_G2r], writes=[Bo])
            yield
            pst = bank_bf(BR)
            for h in range(4):
                P.op("pe", lambda e, h=h: e.transpose(pst[:, h * 128:(h + 1) * 128], o_[:, h * 128:(h + 1) * 128], ident_b[:]), reads=[Bo, B_cstb], writes=[BBR])
            yield
            P.op("act", lambda e: e.copy(out=obT[:, :, gtile * 128:(gtile + 1) * 128], in_=pst[:, 0:512].rearrange("p (h t) -> p h t", h=4)),
                 reads=[BBR], writes=[B_obT[gtile]])
            yield

        def par_(gens):
            active = list(gens)
            while active:
                for g_ in list(active):
                    try:
                        next(g_)
                    except StopIteration:
                        active.remove(g_)
                yield

        def stage_gen(st):
            own = st >= 4
            sp = st % 2
            xTs, BxT = xT[sp], B_xT[sp]
            qT, kT, B_qT, B_kT = qT2[sp], kT2[sp], B_qT2[sp], B_kT2[sp]
            k_tm, v_tm, B_ktm, B_vtm = k_tm2[sp], v_tm2[sp], B_ktm2[sp], B_vtm2[sp]
            beta, gg, B_beta, B_gg = beta2[sp], gg2[sp], B_beta2[sp], B_gg2[sp]
            for t in range(4):
                if own:
                    rows = x_d[(st - 4) * 512 + t * 128:(st - 4) * 512 + (t + 1) * 128, :]
                else:
                    rows = xp_d[st * 512 + t * 128:st * 512 + (t + 1) * 128, :]
                xt.tile(rows, xTs, (t * 128, (t + 1) * 128), BxT, evac=("act" if t % 2 == 0 else "dve"))
                if first_stage[0] and t == 1:
                    early_loads()
                if first_stage[0] and t == 3:
                    late_loads()
                    first_stage[0] = False
                yield
            for t in range(4):
                proj_tm(bank(0)[:, t * 8:(t + 1) * 8], B_bk[0], Wba, B_Wba, 0, 8, xTs, BxT, t * 128)
                yield
            P.op("dve", lambda e: e.tensor_copy(out=ba[:], in_=bank(0)[:, 0:32].rearrange("p (t c) -> p t c", t=4)), reads=[B_bk[0]], writes=[B_ba])
            yield
            P.op("act", lambda e: e.activation(out=beta[:], in_=ba[:, :, 0:4], func=AF.Tanh, scale=0.5), reads=[B_ba], writes=[B_beta])
            P.op("dve", lambda e: e.tensor_tensor(out=gtmp[:], in0=ba[:, :, 4:8], in1=small_s[:, 12:16].unsqueeze(1).to_broadcast([128, 4, 4]), op=ALU.add),
                 reads=[B_ba, B_cst], writes=[B_gt])
            yield
            P.op("dve", lambda e: e.tensor_scalar(out=beta[:], in0=beta[:], scalar1=0.5, scalar2=0.5, op0=ALU.mult, op1=ALU.add), reads=[B_beta], writes=[B_beta])
            P.op("dve", lambda e: e.tensor_scalar(out=nbeta2[sp][:], in0=beta[:], scalar1=-1.0, scalar2=None, op0=ALU.mult), reads=[B_beta], writes=[B_beta])
            order = [4, 5, 6, 7, 0, 1, 2, 3, 8, 9, 10, 11]
            for i0 in range(0, 12, NR):
                yield from par_([chunk_chain(order[i0 + j], xTs, BxT, j) for j in range(NR)])
            P.op("act", lambda e: e.activation(out=gtmp[:], in_=gtmp[:], func=AF.Exp), reads=[B_gt], writes=[B_gt])
            P.op("act", lambda e: e.activation(out=gtmp[:], in_=gtmp[:], func=AF.Ln, bias=1.0), reads=[B_gt], writes=[B_gt])
            yield
            P.op("dve", lambda e: e.tensor_tensor(out=gg[:], in0=gtmp[:], in1=nega_s[:].unsqueeze(1).to_broadcast([128, 4, 4]), op=ALU.mult),
                 reads=[B_gt, B_der], writes=[B_gg])
            yield
            gl4, Bg4, e4, Be4 = gcl4[sp], B_gcl4[sp], eg4[sp], B_eg4[sp]
            ggf = gg[:].rearrange("p t h -> p (t h)")
            P.op("pe", lambda e: e.matmul(bank(0)[:, 0:16], lhsT=U_f[:], rhs=ggf, start=True, stop=True), reads=[B_gg, B_cst], writes=[B_bk[0]])
            P.op("pe", lambda e: e.matmul(bank(0)[:, 16:32], lhsT=ones_f[:], rhs=ggf, start=True, stop=True), reads=[B_gg, B_cst], writes=[B_bk[0]])
            yield
            P.op("dve", lambda e: e.tensor_copy(out=gl4[:, :, 0:4], in_=bank(0)[:, 0:16].rearrange("p (t h) -> p t h", t=4)), reads=[B_bk[0]], writes=[Bg4])
            P.op("dve", lambda e: e.tensor_copy(out=gl4[:, :, 8:12], in_=bank(0)[:, 16:32].rearrange("p (t h) -> p t h", t=4)), reads=[B_bk[0]], writes=[Bg4])
            P.op("dve", lambda e: e.tensor_tensor(out=gl4[:, :, 4:8], in0=gl4[:, :, 8:12], in1=gl4[:, :, 0:4], op=ALU.subtract), reads=[Bg4], writes=[Bg4])
            yield
            P.op("act", lambda e: e.activation(out=e4[:], in_=gl4[:], func=AF.Exp), reads=[Bg4], writes=[Be4])
            yield
            for i0 in range(0, 8, 2):
                yield from par_([norm_chain(qi, sp) for qi in ((4 + i0, 5 + i0) if i0 < 4 else (i0 - 4, i0 - 3))])
            for t in range(4):
                for (srcT, Bsrc, dst, Bdst, evac) in ((kT, B_kT, k_tm, B_ktm, "act"), (vT, B_vT, v_tm, B_vtm, "dve")):
                    pst = bank_bf(0)
                    for h in range(4):
                        P.op("pe", lambda e, h=h, srcT=srcT, pst=pst, t=t: e.transpose(pst[:, h * 128:(h + 1) * 128], srcT[:, h, t * 128:(t + 1) * 128], ident_b[:]),
                             reads=[Bsrc[h], B_cstb], writes=[B_bk[0]])
                    yield
                    if evac == "act":
                        P.op("act", lambda e, dst=dst, pst=pst, t=t: e.copy(out=dst[:, t, :], in_=pst[:, 0:512]), reads=[B_bk[0]], writes=[Bdst[t]])
                    else:
                        P.op("dve", lambda e, dst=dst, pst=pst, t=t: e.tensor_copy(out=dst[:, t, :], in_=pst[:, 0:512]), reads=[B_bk[0]], writes=[Bdst[t]])
                    yield

        def dz_part(st):
            sp = st % 2
            xTs, BxT = xT[sp], B_xT[sp]
            for t in range(4):
                bk_ = PJ_BANK[t % 2]
                proj_tm(bank(bk_), B_bk[bk_], Wz, B_Wz, 0, 512, xTs, BxT, t * 128)
                z, Bz = zs[t % 2], B_zs[t % 2]
                P.op("act", lambda e, z=z, bk_=bk_: e.activation(out=z[:], in_=bank(bk_), func=AF.Silu), reads=[B_bk[bk_]], writes=[Bz])
                P.op("dve", lambda e, z=z, t=t: e.tensor_tensor(out=v4(G2[:, t, :]), in0=v4(z[:]), in1=nw_s[:].unsqueeze(1).to_broadcast([128, 4, 128]), op=ALU.mult),
                     reads=[Bz, B_cst], writes=[B_G2[t]])

        def tiles_gen(st):
            own = st >= 4
            sp = st % 2
            pre_done = [False] * 4
            rec_done = [False] * 4

            def seq_pre(c, lag):
                for _ in range(lag):
                    yield
                for t in (c, c + 2):
                    yield from pre_chain(t, own, c, sp, (lambda t=t: t < 2 or rec_done[t - 2]))
                    pre_done[t] = True

            def seq_rec():
                for t in range(4):
                    while not pre_done[t]:
                        yield
                    gtile = (st - 4) * 4 + t if own else None
                    yield from rec_chain(t, own, t % 2, gtile, sp)
                    rec_done[t] = True

            yield from par_([seq_pre(0, 0), seq_pre(1, 24), seq_rec()])

        st_list = list(range(8))
        run_chains([stage_gen(st_list[0])])
        for i_, st in enumerate(st_list):
            if st >= 4:
                dz_part(st)
            tg_ = tiles_gen(st)
            sg_ = stage_gen(st_list[i_ + 1]) if i_ + 1 < len(st_list) else None
            W_T = 1
            t_done = s_done = False
            while not (t_done and (s_done or sg_ is None)):
                for _ in range(W_T):
                    if not t_done:
                        try:
                            next(tg_)
                        except StopIteration:
                            t_done = True
                if sg_ is not None and not s_done:
                    try:
                        next(sg_)
                    except StopIteration:
                        s_done = True

    def phase_1b():
        R = Region(nc, BASE + 124 * KB, TOP)
        Wq = R.alloc("Wq", [128, 8, 512], BF16)
        Wk = R.alloc("Wk", [128, 8, 256], BF16)
        Wv = R.alloc("Wv", [128, 8, 128], BF16)
        B_Wq, B_Wk, B_Wv = Buf("Wq"), Buf("Wk"), Buf("Wv")
        lw = [P.lane(f"w1b{i}") for i in range(3)]
        def emit_k(e):
            r = []
            for g in range(2):
                for d in range(2):
                    r.append(e.dma_start(out=Wk[:, :, g * 128 + d * 64:g * 128 + (d + 1) * 64], in_=w_in_v[:, :, C_AK + g * 64:C_AK + (g + 1) * 64]))
            return r

        B_b0 = Buf("b0", excl=True)
        xt = XT(R, 0, B_b0)
        xT = [R.alloc(f"xT{i}", [128, 8, 512], BF16) for i in range(2)]
        B_xT = [Buf(f"xT{i}") for i in range(2)]
        xTh = R.alloc("xTh", [128, 8, 128], BF16)
        B_xTh = Buf("xTh")
        aqT = R.alloc("aqT", [128, 4, 512], BF16)
        B_aqT = [Buf(f"aqT{i}") for i in range(4)]
        kkT = R.alloc("kkT", [128, 2, 640], BF16)
        B_kkT = [Buf(f"kkT{i}") for i in range(2)]
        Vaug = R.alloc("Vaug", [128, 5, 2, 80], BF16)
        B_V = [Buf(f"V{i}") for i in range(5)]
        eS = [R.alloc(f"eS{i}", [128, 2, 512], BF16) for i in range(2)]
        B_eS = [Buf(f"eS{i}") for i in range(2)]
        den = R.alloc("den", [128, 2, 4], F32)
        B_den = [Buf("den0"), Buf("den1")]
        oa = [R.alloc(f"oa{i}", [128, 512], BF16) for i in range(2)]
        B_oa = [Buf(f"oa{i}") for i in range(2)]
        B_pj = [Buf("pj0", excl=True), Buf("pj1", excl=True)]
        B_b3 = Buf("b3", excl=True)
        B_S = [Buf("S0", excl=True), Buf("S1", excl=True)]
        B_b2 = Buf("b2", excl=True)
        B_o = [Buf("o0", excl=True), B_b2]

        P.op("pool", lambda e: e.memset(Vaug[:], 1.0), writes=B_V)

        def proj_k_v(xTs, BxT, x0, n, kk_c0, vslot0):
            for g in range(2):
                pj, Bpj = bank(1)[:, 0:n], B_pj[0]
                proj_fm(pj, Bpj, Wk, B_Wk, g * 128, xTs, BxT, x0, n)
                P.op("act", lambda e, g=g, pj=pj: e.copy(out=kkT[:, g, kk_c0:kk_c0 + n], in_=pj), reads=[Bpj], writes=[B_kkT[g]])
            for t in range(n // 128):
                proj_tm(bank(2)[:, 0:128], B_b2, Wv, B_Wv, 0, 128, xTs, BxT, x0 + t * 128)
                P.op("dve", lambda e, t=t: e.tensor_copy(out=Vaug[:, vslot0 + t, :, 0:64], in_=bank(2)[:, 0:128].rearrange("p (g d) -> p g d", g=2)),
                     reads=[B_b2], writes=[B_V[vslot0 + t]])

        xt.tile(xp_d[T_CORE - 128:T_CORE, :], xTh, (0, 128), B_xTh)
        P.op("pool", emit_k, writes=[B_Wk], lane=lw[1], ndma=4)
        load_w(Wv, B_Wv, lw[2], (C_AV, C_AV + 128))
        load_w(Wq, B_Wq, lw[0], (C_AQ, C_AQ + 512))
        proj_k_v(xTh, B_xTh, 0, 128, 0, 0)

        for st in range(4):
            xTs, BxT = xT[st % 2], B_xT[st % 2]
            for t in range(4):
                xt.tile(x_d[st * 512 + t * 128:st * 512 + (t + 1) * 128, :], xTs, (t * 128, (t + 1) * 128), BxT, evac=("act" if t % 2 == 0 else "dve"))
            if st == 1:
                prefetch_1c()
            if st > 0:
                for g in range(2):
                    P.op("pool", lambda e, g=g: e.tensor_copy(out=kkT[:, g, 0:128], in_=kkT[:, g, 512:640]), reads=[B_kkT[g]], writes=[B_kkT[g]])
                P.op("pool", lambda e: e.tensor_copy(out=Vaug[:, 0, :, 0:64], in_=Vaug[:, 4, :, 0:64]), reads=[B_V[4]], writes=[B_V[0]])
            proj_k_v(xTs, BxT, 0, 512, 128, 1)
            for c in range(4):
                pj, Bpj = bank(1), B_pj[0]
                proj_fm(pj, Bpj, Wq, B_Wq, c * 128, xTs, BxT, 0, 512)
                P.op("act", lambda e, c=c, pj=pj: e.copy(out=aqT[:, c, :], in_=pj), reads=[Bpj], writes=[B_aqT[c]])
            def grp_chain(t, g, gtile, o_, Bo):
                Sps = PS[2 + g]
                for kb in range(2):
                    mask = (Mprev0_b if gtile == 0 else Mprev_b) if kb == 0 else Mown_b
                    kc0 = t * 128 + kb * 128
                    for hh in range(2):
                        sp_ = Sps[:, hh * 512 + kb * 256:hh * 512 + (kb + 1) * 256]
                        P.op("pe", lambda e, sp_=sp_, mask=mask: e.matmul(sp_, lhsT=ident_b[:], rhs=mask[:, 0:256], start=True, stop=False), reads=[B_cstb], writes=[B_S[g]])
                    for hh in range(2):
                        sp_ = Sps[:, hh * 512 + kb * 256:hh * 512 + (kb + 1) * 256]
                        P.op("pe", lambda e, sp_=sp_, hh=hh, kc0=kc0: e.matmul(
                            sp_.rearrange("p (c q) -> p c q", c=2),
                            lhsT=kkT[hh * 64:(hh + 1) * 64, g, kc0:kc0 + 128],
                            rhs=aqT[hh * 64:(hh + 1) * 64, 2 * g:2 * g + 2, t * 128:(t + 1) * 128],
                            start=False, stop=True), reads=[B_kkT[g], B_aqT[2 * g], B_aqT[2 * g + 1]], writes=[B_S[g]])
                yield
                for hh in range(2):
                    P.op("act", lambda e, hh=hh: e.activation(out=eS[g][:, hh, :], in_=Sps[:, hh * 512:(hh + 1) * 512], func=AF.Exp, scale=0.125), reads=[B_S[g]], writes=[B_eS[g]])
                yield
                ops_ = (bank(3)[:, 0:320] if g == 0 else bank(2)[:, 128:448]).rearrange("p (h d) -> p h d", h=4)
                for hp in range(4):
                    for kb in range(2):
                        P.op("pe", lambda e, hp=hp, kb=kb: e.matmul(ops_[:, hp, :], lhsT=eS[g][:, hp // 2, kb * 256 + (hp % 2) * 128:kb * 256 + (hp % 2 + 1) * 128],
                                                                    rhs=Vaug[:, t + kb, g, :], start=(kb == 0), stop=(kb == 1)),
                             reads=[B_eS[g], B_V[t + kb]], writes=[B_o[g]])
                yield
                P.op("dve", lambda e: e.tensor_tensor(out=den[:, g, :].rearrange("p (a b) -> p a b", a=2), in0=ops_[:, :, 64].rearrange("p (a b) -> p a b", a=2),
                                                      in1=esink_s[:, g * 4:(g + 1) * 4].rearrange("p (c a) -> p a c", c=2, a=2), op=ALU.add),
                     reads=[B_o[g], B_der], writes=[B_den[g]])
                P.op("dve", lambda e: e.reciprocal(out=den[:, g, :], in_=den[:, g, :]), reads=[B_den[g]], writes=[B_den[g]])
                for a_ in range(2):
                    P.op("dve", lambda e, a_=a_: e.tensor_tensor(
                        out=o_[:, g * 256:(g + 1) * 256].rearrange("p (c a d) -> p a c d", c=2, a=2, d=64)[:, a_],
                        in0=ops_[:, 2 * a_:2 * a_ + 2, 0:64],
                        in1=den[:, g, 2 * a_:2 * a_ + 2].unsqueeze(2).to_broadcast([128, 2, 64]), op=ALU.mult),
                        reads=[B_o[g], B_den[g]], writes=[Bo])
                yield

            def tail_chain(gtile, o_, Bo):
                pst = bank_bf(0)
                for c in range(4):
                    P.op("pe", lambda e, c=c: e.transpose(pst[:, c * 128:(c + 1) * 128], o_[:, c * 128:(c + 1) * 128], ident_b[:]), reads=[Bo, B_cstb], writes=[B_b0])
                yield
                P.op("act", lambda e: e.copy(out=oaT[:, :, gtile * 128:(gtile + 1) * 128], in_=pst[:, 0:512].rearrange("p (h t) -> p h t", h=4)),
                     reads=[B_b0], writes=[B_oaT[gtile]])
                yield

            done = [0] * 4
            tdone = [False] * 4

            def seq_grp(g, lag):
                for _ in range(lag):
                    yield
                for t in range(4):
                    while t >= 2 and not tdone[t - 2]:
                        yield
                    yield from grp_chain(t, g, st * 4 + t, oa[t % 2], B_oa[t % 2])
                    done[t] += 1

            def seq_tail():
                for t in range(4):
                    while done[t] < 2:
                        yield
                    yield from tail_chain(st * 4 + t, oa[t % 2], B_oa[t % 2])
                    tdone[t] = True

            run_chains([seq_grp(0, 0), seq_grp(1, 3), seq_tail()])

    PF = {}

    def prefetch_1c():
        R = Region(nc, BASE + 76 * KB, BASE + 124 * KB)
        Wg = R.alloc("Wg", [128, 8, 2048], BF16)
        Wao = R.alloc("Wao", [128, 4, D], BF16)
        Wdo = R.alloc("Wdo", [128, 4, D], BF16)
        B_Wg = [Buf(f"Wg{i}") for i in range(4)]
        B_Wao, B_Wdo = Buf("Wao"), Buf("Wdo")
        lw = [P.lane(f"w1c{i}") for i in range(6)]
        P.op("pool", lambda e: e.dma_start(out=Wao[:], in_=w_ao_d.rearrange("(kc p) c -> p kc c", p=128)), writes=[B_Wao], lane=lw[4])
        P.op("pool", lambda e: e.dma_start(out=Wdo[:], in_=w_do_d.rearrange("(kc p) c -> p kc c", p=128)), writes=[B_Wdo], lane=lw[5])
        for i in (0, 2, 1, 3):
            def emit(e, i=i):
                return [e.dma_start(out=Wg[:, 2 * j:2 * j + 2, i * 512:(i + 1) * 512], in_=w_in_v[:, 2 * j:2 * j + 2, C_G + i * 512:C_G + (i + 1) * 512]) for j in range(4)]
            P.op("pool", emit, writes=[B_Wg[i]], lane=lw[i], ndma=4)
        PF.update(Wg=Wg, Wao=Wao, Wdo=Wdo, B_Wg=B_Wg, B_Wao=B_Wao, B_Wdo=B_Wdo)

    def phase_1c():
        if not PF:
            prefetch_1c()
        R = Region(nc, BASE + 124 * KB, TOP)
        Wg, Wao, Wdo, B_Wg, B_Wao, B_Wdo = PF["Wg"], PF["Wao"], PF["Wdo"], PF["B_Wg"], PF["B_Wao"], PF["B_Wdo"]
        xt = XT(R, 0, Buf("b0", excl=True))
        xT = [R.alloc(f"xT{i}", [128, 8, 512], BF16) for i in range(2)]
        B_xT = [Buf(f"xT{i}") for i in range(2)]
        ta = [R.alloc(f"ta{i}", [128, 512], F32) for i in range(2)]
        tb = [R.alloc(f"tb{i}", [128, 512], F32) for i in range(2)]
        m1 = [R.alloc(f"m1{i}", [128, 512], F32) for i in range(2)]
        B_ta = [Buf(f"ta{i}") for i in range(2)]
        B_tb = [Buf(f"tb{i}") for i in range(2)]
        B_m1 = [Buf(f"m1{i}") for i in range(2)]
        B_bk = [Buf(f"bk{i}", excl=True) for i in range(8)]
        it = 0
        for st in range(4):
            xTs, BxT = xT[st % 2], B_xT[st % 2]
            for t in range(4):
                xt.tile(x_d[st * 512 + t * 128:st * 512 + (t + 1) * 128, :], xTs, (t * 128, (t + 1) * 128), BxT, evac=("act" if t % 2 == 0 else "dve"))
            if st == 1:
                prefetch_1d()
            tok0 = st * 512
            Bo_a = B_oaT[st * 4:st * 4 + 4]
            Bo_b = B_obT[st * 4:st * 4 + 4]
            for m in range(8):
                u_ = it % 2
                it += 1
                ia, ib, iya, iyb = (1, 2, 3, 4) if u_ == 0 else (5, 6, 7, 4)
                proj_fm(bank(ia), B_bk[ia], Wg, B_Wg[m // 4], m * 128, xTs, BxT, 0, 512)
                proj_fm(bank(ib), B_bk[ib], Wg, B_Wg[2 + m // 4], 1024 + m * 128, xTs, BxT, 0, 512)
                for kc in range(4):
                    P.op("pe", lambda e, kc=kc, iya=iya, m=m, tok0=tok0: e.matmul(bank(iya), lhsT=Wao[:, kc, m * 128:(m + 1) * 128], rhs=oaT[:, kc, tok0:tok0 + 512], start=(kc == 0), stop=(kc == 3)),
                         reads=[B_Wao] + Bo_a, writes=[B_bk[iya]])
                P.op("act", lambda e, u_=u_, ia=ia: e.activation(out=ta[u_][:], in_=bank(ia), func=AF.Tanh, scale=0.5), reads=[B_bk[ia]], writes=[B_ta[u_]])
                P.op("dve", lambda e, u_=u_, iya=iya: e.scalar_tensor_tensor(out=m1[u_][:], in0=ta[u_][:], scalar=1.0, in1=bank(iya), op0=ALU.add, op1=ALU.mult),
                     reads=[B_ta[u_], B_bk[iya]], writes=[B_m1[u_]])
                for kc in range(4):
                    P.op("pe", lambda e, kc=kc, iyb=iyb, m=m, tok0=tok0: e.matmul(bank(iyb), lhsT=Wdo[:, kc, m * 128:(m + 1) * 128], rhs=obT[:, kc, tok0:tok0 + 512], start=(kc == 0), stop=(kc == 3)),
                         reads=[B_Wdo] + Bo_b, writes=[B_bk[iyb]])
                P.op("act", lambda e, u_=u_, ib=ib: e.activation(out=tb[u_][:], in_=bank(ib), func=AF.Tanh, scale=0.5), reads=[B_bk[ib]], writes=[B_tb[u_]])
                P.op("dve", lambda e, u_=u_, iyb=iyb: e.scalar_tensor_tensor(out=tb[u_][:], in0=tb[u_][:], scalar=1.0, in1=bank(iyb), op0=ALU.add, op1=ALU.mult),
                     reads=[B_tb[u_], B_bk[iyb]], writes=[B_tb[u_]])
                P.op("dve", lambda e, u_=u_, m=m, tok0=tok0: e.tensor_tensor(out=mgT[:, m, tok0:tok0 + 512], in0=m1[u_][:], in1=tb[u_][:], op=ALU.add),
                     reads=[B_m1[u_], B_tb[u_]], writes=[B_mgT[st]])

    class LN:
        def __init__(self, R, name, ns=2):
            self.ns = ns
            self.stat = [R.alloc(f"{name}_st{i}", [128, 8], F32) for i in range(ns)]
            self.B_stat = [Buf(f"{name}_st{i}") for i in range(ns)]
            self.junk = R.alloc(f"{name}_junk", [128, D], BF16)
            self.B_junk = Buf(f"{name}_junk")
            self.n = 0

        def tile(self, src_ap, src_bufs, dst_ap, dst_bufs, g_t, b_t, B_lnp, eps_eff):
            i = self.n % self.ns
            self.n += 1
            stat, B_stat, junk, B_junk = self.stat[i], self.B_stat[i], self.junk, self.B_junk
            P.op("dve", lambda e: e.memset(stat[:], 0.0), writes=[B_stat])
            yield
            P.op("act", lambda e: e.activation(out=junk[:], in_=src_ap, func=AF.Identity, accum_out=stat[:, 0:1]), reads=src_bufs, writes=[B_junk, B_stat])
            P.op("act", lambda e: e.activation(out=junk[:], in_=src_ap, func=AF.Square, accum_out=stat[:, 1:2]), reads=src_bufs, writes=[B_junk, B_stat])
            yield
            P.op("dve", lambda e: e.tensor_scalar(out=stat[:, 2:3], in0=stat[:, 0:1], scalar1=1.0 / D, scalar2=None, op0=ALU.mult), reads=[B_stat], writes=[B_stat])
            P.op("dve", lambda e: e.tensor_tensor(out=stat[:, 3:4], in0=stat[:, 2:3], in1=stat[:, 2:3], op=ALU.mult), reads=[B_stat], writes=[B_stat])
            P.op("dve", lambda e: e.scalar_tensor_tensor(out=stat[:, 4:5], in0=stat[:, 1:2], scalar=1.0 / D, in1=stat[:, 3:4], op0=ALU.mult, op1=ALU.subtract), reads=[B_stat], writes=[B_stat])
            yield
            P.op("act", lambda e: e.activation(out=stat[:, 5:6], in_=stat[:, 4:5], func=AF.Ln, bias=eps_eff), reads=[B_stat], writes=[B_stat])
            P.op("act", lambda e: e.activation(out=stat[:, 5:6], in_=stat[:, 5:6], func=AF.Exp, scale=-0.5), reads=[B_stat], writes=[B_stat])
            yield
            P.op("dve", lambda e: e.scalar_tensor_tensor(out=stat[:, 6:7], in0=stat[:, 2:3], scalar=-1.0, in1=stat[:, 5:6], op0=ALU.mult, op1=ALU.mult), reads=[B_stat], writes=[B_stat])
            yield
            P.op("act", lambda e: e.activation(out=dst_ap, in_=src_ap, func=AF.Identity, bias=stat[:, 6:7], scale=stat[:, 5:6]), reads=src_bufs + [B_stat], writes=dst_bufs)
            yield
            P.op("dve", lambda e: e.tensor_tensor(out=dst_ap, in0=dst_ap, in1=g_t, op=ALU.mult), reads=dst_bufs + [B_lnp], writes=dst_bufs)
            yield
            P.op("dve", lambda e: e.tensor_tensor(out=dst_ap, in0=dst_ap, in1=b_t, op=ALU.add), reads=dst_bufs + [B_lnp], writes=dst_bufs)
            yield

    PF1D = {}

    def prefetch_1d():
        R = Region(nc, BASE + 172 * KB, BASE + 196 * KB)
        Wo = R.alloc("Wo", [128, 8, D], BF16)
        B_Wo = Buf("Wo")
        lwo = P.lane("wo")
        P.op("pool", lambda e: [e.dma_start(out=Wo[:, 2 * j:2 * j + 2, :], in_=w_out_d.rearrange("(kc p) c -> p kc c", p=128)[:, 2 * j:2 * j + 2, :]) for j in range(4)],
             writes=[B_Wo], lane=lwo, ndma=4)
        lnp = R.alloc("lnp1", [128, 2, D], F32)
        B_lnp = Buf("lnp1")
        llnp = P.lane("lnp1")
        P.op("sp", lambda e: e.dma_start(out=lnp[:].rearrange("p a d -> p (a d)"), in_=lnp_d[:, 0:2 * D]), writes=[B_lnp], lane=llnp)
        PF1D.update(Wo=Wo, B_Wo=B_Wo, lnp=lnp, B_lnp=B_lnp)

    def phase_1d():
        if not PF1D:
            prefetch_1d()
        Wo, B_Wo, lnp, B_lnp = PF1D["Wo"], PF1D["B_Wo"], PF1D["lnp"], PF1D["B_lnp"]
        NW = 4
        R = Region(nc, BASE + 12 * KB, BASE + 44 * KB)
        R3 = Region(nc, BASE + 196 * KB, TOP)
        xs = [R.alloc(f"xs{i}", [128, D], F32) for i in range(NW)]
        B_xs = [Buf(f"xs{i}") for i in range(NW)]
        L_xs = [P.lane(f"xs{i}") for i in range(NW)]
        pre = [R.alloc(f"pre{i}", [128, D], F32) for i in range(NW)]
        B_pre = [Buf(f"pre{i}") for i in range(NW)]
        x1b = [R3.alloc(f"x1b{i}", [128, D], BF16) for i in range(NW)]
        B_x1b = [Buf(f"x1b{i}") for i in range(NW)]
        ln = LN(R3, "ln1", ns=NW)
        B_mix = [Buf(f"mix{i}", excl=True) for i in range(NW)]
        def tile_gen(t):
            u_ = t % NW
            P.op("sp", lambda e: e.dma_start(out=xs[u_][:], in_=x_d[t * 128:(t + 1) * 128, :]), writes=[B_xs[u_]], lane=L_xs[u_])
            mix = PS[u_]
            for half in range(2):
                for kc in range(8):
                    P.op("pe", lambda e, kc=kc, half=half: e.matmul(mix[:, half * 512:(half + 1) * 512], lhsT=mgT[:, kc, t * 128:(t + 1) * 128], rhs=Wo[:, kc, half * 512:(half + 1) * 512],
                                                                     start=(kc == 0), stop=(kc == 7)), reads=[B_mgT[t // 4], B_Wo], writes=[B_mix[u_]])
            yield
            P.op("dve", lambda e: e.scalar_tensor_tensor(out=pre[u_][:], in0=mix[:, :], scalar=0.5 / ALPHA, in1=xs[u_][:], op0=ALU.mult, op1=ALU.add),
                 reads=[B_mix[u_], B_xs[u_]], writes=[B_pre[u_]])
            yield
            yield from ln.tile(pre[u_][:], [B_pre[u_]], acc[:, t, :], [B_acc[t]], lnp[:, 0, :], lnp[:, 1, :], B_lnp, LN_EPS / (ALPHA * ALPHA))
            P.op("act", lambda e: e.copy(out=x1b[u_][:], in_=acc[:, t, :]), reads=[B_acc[t]], writes=[B_x1b[u_]])
            yield
            pst = mix[:, 0:512].bitcast(BF16)
            for kc in range(8):
                P.op("pe", lambda e, kc=kc: e.transpose(pst[:, kc * 128:(kc + 1) * 128], x1b[u_][:, kc * 128:(kc + 1) * 128], ident_b[:]), reads=[B_x1b[u_], B_cstb], writes=[B_mix[u_]])
            yield
            P.op("dve", lambda e: e.tensor_copy(out=x1T[:, :, t * 128:(t + 1) * 128], in_=pst.rearrange("p (k t) -> p k t", k=8)), reads=[B_mix[u_]], writes=[B_x1T[t]])
            yield

        run_window([tile_gen(t) for t in range(NT)], NW, stagger=4)

    def phase_2():
        RA = Region(nc, BASE + 12 * KB, BASE + 76 * KB)
        RB = Region(nc, BASE + 172 * KB, TOP)
        wu = [RA.alloc(f"wu{i}", [128, 8, 512], BF16) for i in range(2)]
        wd = [RA.alloc(f"wd{i}", [128, 4, D], BF16) for i in range(2)]
        B_wu = [Buf(f"wu{i}") for i in range(2)]
        B_wd = [Buf(f"wd{i}") for i in range(2)]
        L_wu = [P.lane(f"wu{i}") for i in range(2)]
        L_wd = [P.lane(f"wd{i}") for i in range(2)]
        hr = [RA.alloc(f"hr{i}", [128, 512], F32) for i in range(2)]
        B_hr = [Buf(f"hr{i}") for i in range(2)]
        hT = [RA.alloc(f"hT{i}", [128, 4, 512], BF16) for i in range(2)]
        B_hT = [[Buf(f"hT{i}_{m}") for m in range(4)] for i in range(2)]
        lnp = RA.alloc("lnp2", [128, 2, D], F32)
        B_lnp = Buf("lnp2")
        llnp = P.lane("lnp2")
        P.op("sp", lambda e: e.dma_start(out=lnp[:].rearrange("p a d -> p (a d)"), in_=lnp_d[:, 2 * D:4 * D]), writes=[B_lnp], lane=llnp)
        ln = LN(RB, "ln2")
        ost = [RB.alloc(f"ost{i}", [128, D], F32) for i in range(2)]
        B_ost = [Buf(f"ost{i}") for i in range(2)]
        L_ost = [P.lane(f"ost{i}") for i in range(2)]
        B_hp = [Buf("hp0", excl=True), Buf("hp1", excl=True)]
        B_y = [Buf("y0", excl=True), Buf("y1", excl=True)]
        w_up_v = w_up_d.rearrange("(kc p) c -> p kc c", p=128)
        w_dn_v = w_down_d.rearrange("(kc p) c -> p kc c", p=128)
        NF = 8
        pending = []
        ost4 = [RB.alloc(f"ostx{i}", [128, D], F32) for i in range(2)]
        ost_all = ost + ost4
        B_ost_all = B_ost + [Buf("ostx0"), Buf("ostx1")]
        L_ost_all = L_ost + [P.lane("ostx0"), P.lane("ostx1")]

        def ln2_chain(t):
            u_ = t % 4
            yield from ln.tile(acc[:, t, :], [B_acc[t]], ost_all[u_][:], [B_ost_all[u_]], lnp[:, 0, :], lnp[:, 1, :], B_lnp, LN_EPS / (ALPHA * ALPHA))
            P.op("sp", lambda e: e.dma_start(out=out_d[t * 128:(t + 1) * 128, :], in_=ost_all[u_][:]), reads=[B_ost_all[u_]], lane=L_ost_all[u_])
            yield

        def tick():
            for g_ in list(pending[:2]):
                try:
                    next(g_)
                except StopIteration:
                    pending.remove(g_)

        for f in range(NF):
            s = f % 2
            P.op("pool", lambda e, f=f, s=s: [e.dma_start(out=wu[s][:, 2 * j:2 * j + 2, :], in_=w_up_v[:, 2 * j:2 * j + 2, f * 512:(f + 1) * 512]) for j in range(4)],
                 writes=[B_wu[s]], lane=L_wu[s], ndma=4)
            P.op("pool", lambda e, f=f, s=s: [e.dma_start(out=wd[s][:, 2 * j:2 * j + 2, :], in_=w_dn_v[:, 4 * f + 2 * j:4 * f + 2 * j + 2, :]) for j in range(2)],
                 writes=[B_wd[s]], lane=L_wd[s], ndma=2)
            for tg in range(4):
                hs = tg % 2
                for mt in range(4):
                    hp, Bhp = bank(mt % 2), B_hp[mt % 2]
                    for kc in range(8):
                        P.op("pe", lambda e, kc=kc, mt=mt, hp=hp, s=s, tg=tg: e.matmul(hp, lhsT=wu[s][:, kc, mt * 128:(mt + 1) * 128], rhs=x1T[:, kc, tg * 512:(tg + 1) * 512], start=(kc == 0), stop=(kc == 7)),
                             reads=[B_wu[s]] + B_x1T[tg * 4:tg * 4 + 4], writes=[Bhp])
                    r_, Br = hr[mt % 2], B_hr[mt % 2]
                    P.op("act", lambda e, r_=r_, hp=hp: e.activation(out=r_[:], in_=hp, func=AF.Relu), reads=[Bhp], writes=[Br])
                    P.op("act", lambda e, r_=r_, hs=hs, mt=mt: e.activation(out=hT[hs][:, mt, :], in_=r_[:], func=AF.Square), reads=[Br], writes=[B_hT[hs][mt]])
                    tick()
                for tl in range(4):
                    t = tg * 4 + tl
                    y, By = PS[1 + tl % 2], B_y[tl % 2]
                    for half in range(2):
                        for mt in range(4):
                            P.op("pe", lambda e, mt=mt, half=half, y=y, hs=hs, tl=tl, s=s: e.matmul(y[:, half * 512:(half + 1) * 512], lhsT=hT[hs][:, mt, tl * 128:(tl + 1) * 128], rhs=wd[s][:, mt, half * 512:(half + 1) * 512],
                                                                                             start=(mt == 0), stop=(mt == 3)), reads=[B_hT[hs][mt], B_wd[s]], writes=[By])
                    P.op("dve", lambda e, t=t, y=y: e.scalar_tensor_tensor(out=acc[:, t, :], in0=y[:, :], scalar=1.0 / ALPHA, in1=acc[:, t, :], op0=ALU.mult, op1=ALU.add),
                         reads=[By, B_acc[t]], writes=[B_acc[t]])
                    if f == NF - 1:
                        pending.append(ln2_chain(t))
                    tick()

        while pending:
            tick()

    def dump_bf(name, src, bufs, shape):
        d = ddump(name, shape, BF16)
        l = P.lane("dbg_" + name)
        P.op("sp", lambda e: e.dma_start(out=d, in_=src), reads=bufs, lane=l)

    def dump_f(name, src, bufs, shape):
        d = ddump(name, shape, F32)
        l = P.lane("dbg_" + name)
        P.op("sp", lambda e: e.dma_start(out=d, in_=src), reads=bufs, lane=l)

    phases = [("1a", phase_1a), ("1b", phase_1b), ("1c", phase_1c), ("1d", phase_1d), ("2", phase_2)]
    for name, fn in phases:
        if stop_after == "0":
            break
        fn()
        P.barrier()
        if dbg:
            if name == "1a":
                dump_bf("d_obT", obT[:], B_obT, [128, 4, T_CORE])
            if name == "1b":
                dump_bf("d_oaT", oaT[:], B_oaT, [128, 4, T_CORE])
            if name == "1c":
                dump_bf("d_mgT", mgT[:], B_mgT, [128, 8, T_CORE])
            if name == "1d":
                dump_f("d_acc", acc[:], B_acc, [128, NT, D])
            P.barrier()
        if stop_after == name:
            break

    with nc.Block() as block:
        P.finalize(block)
    return nc, dbg_out


def _consts():
    p = np.arange(128)[:, None]
    f = np.arange(128)[None, :]
    ident = (p == f).astype(np.float32)
    U = (p <= f).astype(np.float32)
    ones = np.ones((128, 128), np.float32)
    MD = np.where(p <= f, 0.0, NEG).astype(np.float32)
    Mown = np.tile(np.where(p <= f, 0.0, NEG).astype(np.float32), (1, 4))
    Mprev = np.tile(np.where(p > f, 0.0, NEG).astype(np.float32), (1, 4))
    sel = np.zeros((128, 4, 128), np.float32)
    for h in range(4):
        sel[h, h, :] = 1.0
    MD4 = np.tile(np.where(p < f, 0.0, NEG).astype(np.float32), (1, 4))
    negsel = np.zeros((128, 4, 128), np.float32)
    blk = np.zeros((128, 4, 128), np.float32)
    for h in range(4):
        negsel[h, h, :] = -1.0
        blk[h, h, :] = 1.0
    cst = np.concatenate([ident, U, ones, MD, Mown, Mprev, sel.reshape(128, 512), MD4, negsel.reshape(128, 512), blk.reshape(128, 512)], axis=1)
    return np.ascontiguousarray(cst), Mprev


def make_in_maps(x, w_in, conv_w, attn_sinks, dn_a_log, dn_dt_bias, dn_norm_w, w_attn_out, w_dn_out, w_out,
                 ln1_g, ln1_b, w_up, w_down, ln2_g, ln2_b):
    f32 = np.float32
    x = np.asarray(x, f32)
    cst, Mprev = _consts()
    cw = np.ascontiguousarray(np.asarray(conv_w, f32)[0].reshape(4, 12, 128).transpose(2, 1, 0).reshape(128, 48))
    small = np.concatenate([np.asarray(attn_sinks, f32)[0], np.asarray(dn_a_log, f32)[0], np.asarray(dn_dt_bias, f32)[0]])
    small = np.ascontiguousarray(np.broadcast_to(small[None, :], (128, 16)))
    nw = np.ascontiguousarray(np.broadcast_to(np.asarray(dn_norm_w, f32)[0][None, :], (128, 128)))
    lnp = np.concatenate([np.asarray(a, f32)[0] for a in (ln1_g, ln1_b, ln2_g, ln2_b)])
    lnp = np.ascontiguousarray(np.broadcast_to(lnp[None, :], (128, 4 * D)))
    shared = {
        "cst": cst, "cw": cw, "small": small, "nw": nw, "lnp": lnp,
        "w_in": np.ascontiguousarray(np.asarray(w_in, f32)[0]),
        "w_ao": np.ascontiguousarray(np.asarray(w_attn_out, f32)[0]),
        "w_do": np.ascontiguousarray(np.asarray(w_dn_out, f32)[0]),
        "w_out": np.ascontiguousarray(np.asarray(w_out, f32)[0]),
        "w_up": np.ascontiguousarray(np.asarray(w_up, f32)[0]),
        "w_down": np.ascontiguousarray(np.asarray(w_down, f32)[0]),
    }
    zeros = np.zeros((T_CORE, D), f32)
    allneg = np.full((128, 512), NEG, f32)
    maps = []
    for c in range(8):
        b, half = c // 2, c % 2
        m = dict(shared)
        m["x"] = np.ascontiguousarray(x[b, half * T_CORE:(half + 1) * T_CORE])
        m["xp"] = np.ascontiguousarray(x[b, 0:T_CORE]) if half == 1 else zeros
        m["mprev0"] = Mprev if half == 1 else allneg
        maps.append(m)
    return maps


_NC_CACHE = {}


def kernel(**inputs):
    if "nc" not in _NC_CACHE:
        _NC_CACHE["nc"] = build()[0]
    nc = _NC_CACHE["nc"]
    maps = make_in_maps(**inputs)
    res = run_bass_kernel_spmd(nc, maps, core_ids=list(range(8)))
    out = np.empty((4, 4096, D), np.float32)
    for c in range(8):
        b, half = c // 2, c % 2
        out[b, half * T_CORE:(half + 1) * T_CORE] = res.results[c]["out"]
    return out
```

```python
import numpy as np
import concourse.bass as bass
import concourse.mybir as mybir
from concourse.bass_utils import run_bass_kernel_spmd

F32 = mybir.dt.float32
BF16 = mybir.dt.bfloat16
AF = mybir.ActivationFunctionType
ALU = mybir.AluOpType

D = 1024
T_CORE = 2048
NT = 16
NEG = -30000.0
ALPHA = 2.0 ** 0.25
LN_EPS = 1e-5
RMS_EPS = 1e-6
IN_WIDTH = 4872
C_AQ, C_AK, C_AV, C_DQKV, C_DZ, C_DB, C_DA, C_G = 0, 512, 640, 768, 2304, 2816, 2820, 2824


class Buf:
    __slots__ = ("name", "w", "r", "excl")

    def __init__(self, name="", excl=False):
        self.name = name
        self.w = None
        self.r = {}
        self.excl = excl

    def _addr(self, tok):
        k = ("eng", tok[1]) if tok[0] == "eng" else ("dma", id(tok[1]))
        old = self.r.get(k)
        if old is None or old[2] < tok[2]:
            self.r[k] = tok


class Lane:
    def __init__(self, sem):
        self.sem = sem
        self.count = 0


class _Op:
    __slots__ = ("emit", "deps", "lane", "signal", "cnt")

    def __init__(self, emit, deps, lane):
        self.emit = emit
        self.deps = deps
        self.lane = lane
        self.signal = False
        self.cnt = 0


class Prog:
    ENGS = ("pe", "act", "dve", "pool", "sp")

    def __init__(self, nc):
        self.nc = nc
        self.ops = {e: [] for e in self.ENGS}
        self.sems = {e: nc.alloc_semaphore("s_" + e) for e in self.ENGS}
        self.lanes = []
        self.pending = {e: [] for e in self.ENGS}

    def lane(self, name):
        l = Lane(self.nc.alloc_semaphore("l_" + name))
        self.lanes.append(l)
        return l

    def barrier(self):
        toks = []
        for e in self.ENGS:
            for i in range(len(self.ops[e]) - 1, -1, -1):
                if self.ops[e][i].lane is None:
                    toks.append(("eng", e, i))
                    break
        for l in self.lanes:
            if l.count > 0:
                toks.append(("dma", l, l.count))
        for e in self.ENGS:
            self.pending[e] = list(toks)

    def op(self, eng, emit, reads=(), writes=(), lane=None, ndma=1):
        idx = len(self.ops[eng])
        deps = {}

        def add(tok):
            if tok is None:
                return
            if tok[0] == "eng":
                if tok[1] == eng and eng in ("pe", "sp"):
                    return
                k = ("eng", tok[1])
            else:
                k = ("dma", id(tok[1]))
            old = deps.get(k)
            if old is None or old[2] < tok[2]:
                deps[k] = tok

        for t in self.pending[eng]:
            add(t)
        self.pending[eng] = []
        for b in reads:
            add(b.w)
            if b.excl:
                for t in b.r.values():
                    if not (t[0] == "eng" and t[1] == eng):
                        add(t)
        for b in writes:
            add(b.w)
            for t in b.r.values():
                add(t)
        if lane is not None:
            lane.count += 16 * ndma
            tok = ("dma", lane, lane.count)
        else:
            tok = ("eng", eng, idx)
        for b in reads:
            b._addr(tok)
        for b in writes:
            b.w = tok
            b.r = {}
        self.ops[eng].append(_Op(emit, list(deps.values()), lane))
        return tok

    def finalize(self, block):
        for e in self.ENGS:
            for o in self.ops[e]:
                for d in o.deps:
                    if d[0] == "eng":
                        self.ops[d[1]][d[2]].signal = True
        for e in self.ENGS:
            c = 0
            for o in self.ops[e]:
                if o.signal:
                    c += 1
                o.cnt = c
        final_lanes = [(l.sem, l.count) for l in self.lanes if l.count > 0]
        final_eng = {e: (self.ops[e][-1] if self.ops[e] else None) for e in self.ENGS}

        def run(ename, eng):
            waited = {}
            for o in self.ops[ename]:
                for d in o.deps:
                    if d[0] == "eng":
                        sem = self.sems[d[1]]
                        val = self.ops[d[1]][d[2]].cnt
                        k = ("e", d[1])
                    else:
                        sem = d[1].sem
                        val = d[2]
                        k = ("l", id(d[1]))
                    if waited.get(k, 0) >= val:
                        continue
                    waited[k] = val
                    eng.wait_ge(sem, val)
                ins = o.emit(eng)
                if o.lane is not None:
                    if not isinstance(ins, (list, tuple)):
                        ins = [ins]
                    for i_ in ins:
                        i_.then_inc(o.lane.sem, 16)
                elif o.signal:
                    ins.then_inc(self.sems[ename], 1)
            if ename == "sp":
                for sem, cnt in final_lanes:
                    eng.wait_ge(sem, cnt)

        block.tensor(lambda e: run("pe", e))
        block.scalar(lambda e: run("act", e))
        block.vector(lambda e: run("dve", e))
        block.gpsimd(lambda e: run("pool", e))
        block.sync(lambda e: run("sp", e))


class Region:
    LO, HI = 16512, 229376
    _n = 0

    def __init__(self, nc, lo, hi):
        assert Region.LO <= lo <= hi <= Region.HI, (lo, hi)
        self.nc, self.lo, self.hi, self.p = nc, lo, hi, lo

    def alloc(self, name, shape, dt):
        esz = 2 if dt == BF16 else 4
        n = esz
        for s in shape[1:]:
            n *= s
        off = (self.p + 63) // 64 * 64
        assert off + n <= self.hi, f"region overflow {name}: need {off + n - self.lo} of {self.hi - self.lo}"
        self.p = off + n
        Region._n += 1
        return self.nc.alloc_sbuf_tensor_at(f"{name}_{Region._n}", list(shape), dt, offset=off)


def build(dbg=False, stop_after=None):
    nc = bass.Bass("TRN2", target_bir_lowering=False)
    P = Prog(nc)
    KB = 1024
    BASE = Region.LO

    def din(name, shape):
        return nc.dram_tensor(name, list(shape), F32, kind="ExternalInput").ap()

    x_d = din("x", [T_CORE, D])
    xp_d = din("xp", [T_CORE, D])
    mprev0_d = din("mprev0", [128, 512])
    cst_d = din("cst", [128, 3584])
    cw_d = din("cw", [128, 48])
    small_d = din("small", [128, 16])
    nw_d = din("nw", [128, 128])
    lnp_d = din("lnp", [128, 4 * D])
    w_in_d = din("w_in", [D, IN_WIDTH])
    w_ao_d = din("w_ao", [512, D])
    w_do_d = din("w_do", [512, D])
    w_out_d = din("w_out", [D, D])
    w_up_d = din("w_up", [D, 4 * D])
    w_down_d = din("w_down", [4 * D, D])
    out_d = nc.dram_tensor("out", [T_CORE, D], F32, kind="ExternalOutput").ap()
    dbg_out = {}

    def ddump(name, shape, dt=F32):
        dbg_out[name] = nc.dram_tensor(name, list(shape), dt, kind="ExternalOutput").ap()
        return dbg_out[name]

    w_in_v = w_in_d.rearrange("(kc p) c -> p kc c", p=128)

    PS = [nc.alloc_psum_tensor(f"ps{i}", [128, 1024], F32) for i in range(4)]

    def bank(i):
        return PS[i // 2][:, (i % 2) * 512:(i % 2 + 1) * 512]

    def bank_bf(i):
        return bank(i).bitcast(BF16)

    RG = Region(nc, BASE, BASE + 12 * KB)
    ident_f = RG.alloc("ident_f", [128, 128], F32)
    U_f = RG.alloc("U_f", [128, 128], F32)
    ones_f = RG.alloc("ones_f", [128, 128], F32)
    MD_f = RG.alloc("MD_f", [128, 128], F32)
    sel_f = RG.alloc("sel_f", [128, 4, 128], F32)
    ident_b = RG.alloc("ident_b", [128, 128], BF16)
    ones_b = RG.alloc("ones_b", [128, 128], BF16)
    Mown_b = RG.alloc("Mown_b", [128, 512], BF16)
    Mprev_b = RG.alloc("Mprev_b", [128, 512], BF16)
    Mprev0_b = RG.alloc("Mprev0_b", [128, 512], BF16)
    cw_s = RG.alloc("cw_s", [128, 12, 4], F32)
    small_s = RG.alloc("small_s", [128, 16], F32)
    nega_s = RG.alloc("nega_s", [128, 4], F32)
    esink_s = RG.alloc("esink_s", [128, 8], F32)
    nw_s = RG.alloc("nw_s", [128, 128], F32)
    B_cst = Buf("cst")
    L_cst = P.lane("cst")

    def ld_consts(e):
        return [
            e.dma_start(out=ident_f[:], in_=cst_d[:, 0:128]),
            e.dma_start(out=U_f[:], in_=cst_d[:, 128:256]),
            e.dma_start(out=ones_f[:], in_=cst_d[:, 256:384]),
            e.dma_start(out=MD_f[:], in_=cst_d[:, 384:512]),
            e.dma_start(out=sel_f[:], in_=cst_d[:, 1536:2048].rearrange("p (h f) -> p h f", h=4)),
            e.dma_start(out=cw_s[:], in_=cw_d.rearrange("p (c t) -> p c t", c=12)),
            e.dma_start(out=small_s[:], in_=small_d),
            e.dma_start(out=nw_s[:], in_=nw_d),
        ]
    P.op("sp", ld_consts, writes=[B_cst], lane=L_cst, ndma=8)
    B_cstb = Buf("cstb")
    L_cstb = P.lane("cstb")

    def ld_consts_b(e):
        return [
            e.dma_start(out=ident_b[:], in_=cst_d[:, 0:128]),
            e.dma_start(out=ones_b[:], in_=cst_d[:, 256:384]),
            e.dma_start(out=Mown_b[:], in_=cst_d[:, 512:1024]),
            e.dma_start(out=Mprev_b[:], in_=cst_d[:, 1024:1536]),
            e.dma_start(out=Mprev0_b[:], in_=mprev0_d),
        ]
    P.op("pool", ld_consts_b, writes=[B_cstb], lane=L_cstb, ndma=5)
    B_der = Buf("derived")
    P.op("act", lambda e: e.activation(out=nega_s[:], in_=small_s[:, 8:12], func=AF.Exp), reads=[B_cst], writes=[B_der])
    P.op("act", lambda e: e.activation(out=esink_s[:], in_=small_s[:, 0:8], func=AF.Exp), reads=[B_cst], writes=[B_der])
    P.op("dve", lambda e: e.tensor_scalar(out=nega_s[:], in0=nega_s[:], scalar1=-1.0, scalar2=None, op0=ALU.mult), reads=[B_der], writes=[B_der])

    R_OB = (BASE + 12 * KB, BASE + 28 * KB)
    R_OA = (BASE + 28 * KB, BASE + 44 * KB)
    R_MG = (BASE + 44 * KB, BASE + 76 * KB)
    R_ACC = (BASE + 76 * KB, BASE + 140 * KB)
    R_X1T = (BASE + 140 * KB, BASE + 172 * KB)
    TOP = Region.HI
    obT = Region(nc, *R_OB).alloc("obT", [128, 4, T_CORE], BF16)
    oaT = Region(nc, *R_OA).alloc("oaT", [128, 4, T_CORE], BF16)
    mgT = Region(nc, *R_MG).alloc("mgT", [128, 8, T_CORE], BF16)
    acc = Region(nc, *R_ACC).alloc("acc", [128, NT, D], F32)
    x1T = Region(nc, *R_X1T).alloc("x1T", [128, 8, T_CORE], BF16)
    B_obT = [Buf(f"obT{t}") for t in range(NT)]
    B_oaT = [Buf(f"oaT{t}") for t in range(NT)]
    B_mgT = [Buf(f"mgT{s}") for s in range(4)]
    B_acc = [Buf(f"acc{t}") for t in range(NT)]
    B_x1T = [Buf(f"x1T{t}") for t in range(NT)]

    def load_w(dst, dst_buf, lane, src_cols, eng="pool"):
        c0, c1 = src_cols

        def emit(e):
            return [e.dma_start(out=dst[:, 2 * j:2 * j + 2, :], in_=w_in_v[:, 2 * j:2 * j + 2, c0:c1]) for j in range(4)]
        P.op(eng, emit, writes=[dst_buf], lane=lane, ndma=4)

    class XT:
        _cnt = [0]

        def __init__(self, R, psbank, B_ps, nslots=2):
            XT._cnt[0] += 1
            self.xb = [R.alloc(f"xb{i}", [128, D], BF16) for i in range(nslots)]
            self.B_xb = [Buf(f"xb{i}") for i in range(nslots)]
            self.L_xb = [P.lane(f"xb{i}_{XT._cnt[0]}") for i in range(nslots)]
            self.n = 0
            self.psbank = psbank
            self.B_ps = B_ps
            self.nslots = nslots

        def tile(self, src_rows, dst, dst_cols, dst_buf, evac="act"):
            s = self.n % self.nslots
            self.n += 1
            xb, Bx, Lx = self.xb[s], self.B_xb[s], self.L_xb[s]
            P.op("pool", lambda e: e.dma_start(out=xb[:], in_=src_rows), writes=[Bx], lane=Lx)
            pst = bank_bf(self.psbank)
            for kc in range(8):
                P.op("pe", lambda e, kc=kc: e.transpose(pst[:, kc * 128:(kc + 1) * 128], xb[:, kc * 128:(kc + 1) * 128], ident_b[:]),
                     reads=[Bx, B_cstb], writes=[self.B_ps])
            c0, c1 = dst_cols
            src = pst.rearrange("p (k t) -> p k t", k=8)
            if evac == "act":
                P.op("act", lambda e: e.copy(out=dst[:, :, c0:c1], in_=src), reads=[self.B_ps], writes=[dst_buf])
            else:
                P.op("dve", lambda e: e.tensor_copy(out=dst[:, :, c0:c1], in_=src), reads=[self.B_ps], writes=[dst_buf])

    def proj_fm(ps_ap, ps_buf, W, wbuf, wc0, xT, xbuf, x0, n):
        for kc in range(8):
            P.op("pe", lambda e, kc=kc: e.matmul(ps_ap, lhsT=W[:, kc, wc0:wc0 + 128], rhs=xT[:, kc, x0:x0 + n],
                                                 start=(kc == 0), stop=(kc == 7)),
                 reads=[wbuf, xbuf], writes=[ps_buf])

    def proj_tm(ps_ap, ps_buf, W, wbuf, wc0, ncols, xT, xbuf, x0):
        for kc in range(8):
            P.op("pe", lambda e, kc=kc: e.matmul(ps_ap, lhsT=xT[:, kc, x0:x0 + 128], rhs=W[:, kc, wc0:wc0 + ncols],
                                                 start=(kc == 0), stop=(kc == 7)),
                 reads=[wbuf, xbuf], writes=[ps_buf])

    def run_chains(gens):
        active = list(gens)
        while active:
            for g_ in list(active):
                try:
                    next(g_)
                except StopIteration:
                    active.remove(g_)

    def run_window(gens, width, stagger=1):
        gens = list(gens)
        active = []
        since = stagger
        while gens or active:
            if gens and len(active) < width and since >= stagger:
                active.append(gens.pop(0))
                since = 0
            since += 1
            for g_ in list(active):
                try:
                    next(g_)
                except StopIteration:
                    active.remove(g_)

    def phase_1a():
        R = Region(nc, BASE + 28 * KB, TOP)
        Wd = R.alloc("Wd", [128, 8, 1536], BF16)
        Wba = R.alloc("Wba", [128, 8, 8], BF16)
        Wz = R.alloc("Wz", [128, 8, 512], BF16)
        B_Wd = [Buf("Wdq"), Buf("Wdk"), Buf("Wdv")]
        B_Wba, B_Wz = Buf("Wba"), Buf("Wz")
        lw = [P.lane(f"w1a{i}") for i in range(7)]

        def load_wd(i):
            def emit(e, i=i):
                return [e.dma_start(out=Wd[:, 2 * j:2 * j + 2, i * 512:(i + 1) * 512],
                                    in_=w_in_v[:, 2 * j:2 * j + 2, C_DQKV + i * 512:C_DQKV + (i + 1) * 512]) for j in range(4)]
            P.op("pool", emit, writes=[B_Wd[i]], lane=lw[i], ndma=4)

        MD4_b = R.alloc("MD4_b", [128, 512], BF16)
        nsb_f = R.alloc("nsb_f", [128, 1024], F32)
        B_c1a, B_c1b = Buf("c1a"), Buf("c1b")
        P.op("sp", lambda e: e.dma_start(out=nsb_f[:], in_=cst_d[:, 2560:3584]), writes=[B_c1b], lane=lw[6])

        def early_loads():
            P.op("pool", lambda e: e.dma_start(out=Wba[:], in_=w_in_v[:, :, C_DB:C_DB + 8]), writes=[B_Wba], lane=lw[3])
            load_wd(1)

        def late_loads():
            load_wd(0)
            load_wd(2)
            P.op("pool", lambda e: e.dma_start(out=MD4_b[:], in_=cst_d[:, 2048:2560]), writes=[B_c1a], lane=lw[5])

            def emit_z(e):
                return [e.dma_start(out=Wz[:, 2 * j:2 * j + 2, :], in_=w_in_v[:, 2 * j:2 * j + 2, C_DZ:C_DZ + 512]) for j in range(4)]
            P.op("pool", emit_z, writes=[B_Wz], lane=lw[4], ndma=4)
        first_stage = [True]
        negsel = nsb_f[0:4, 0:512]
        blk = nsb_f[0:4, 512:1024]

        B_bk = [Buf(f"bank{i}", excl=True) for i in range(8)]
        xt = XT(R, 0, B_bk[0])
        xT1 = R.alloc("xT", [128, 8, 512], BF16)
        xT = [xT1, xT1]
        B_xT1 = Buf("xT")
        B_xT = [B_xT1, B_xT1]
        NR = 2
        raw = [R.alloc(f"raw{i}", [128, 515], F32) for i in range(NR)]
        B_raw = [Buf(f"raw{i}") for i in range(NR)]
        cacc = [R.alloc(f"cacc{i}", [128, 512], F32) for i in range(NR)]
        B_cacc = [Buf(f"cacc{i}") for i in range(NR)]
        carry = R.alloc("carry", [128, 12, 3], F32)
        B_carry = [Buf(f"carry{i}") for i in range(12)]
        s_qk = R.alloc("s_qk", [128, 8, 512], F32)
        B_sqk = [Buf(f"sqk{i}") for i in range(8)]
        vT = R.alloc("vT", [128, 4, 512], BF16)
        B_vT = [Buf(f"vT{i}") for i in range(4)]
        qT2 = [R.alloc(f"qT_{i}", [128, 4, 512], BF16) for i in range(2)]
        kT2 = [R.alloc(f"kT_{i}", [128, 4, 512], BF16) for i in range(2)]
        B_qT2 = [[Buf(f"qT{j}{i}") for i in range(4)] for j in range(2)]
        B_kT2 = [[Buf(f"kT{j}{i}") for i in range(4)] for j in range(2)]
        sq_b = [R.alloc(f"sq_b{i}", [128, 512], BF16) for i in range(2)]
        B_sqb = [Buf(f"sqb{i}") for i in range(2)]
        rs = [R.alloc(f"rs{i}", [128, 512], F32) for i in range(2)]
        B_rs = [Buf(f"rs{i}") for i in range(2)]
        k_tm2 = [R.alloc(f"k_tm{i}", [128, 4, 512], BF16) for i in range(2)]
        v_tm2 = [R.alloc(f"v_tm{i}", [128, 4, 512], BF16) for i in range(2)]
        B_ktm2 = [[Buf(f"ktm{j}{i}") for i in range(4)] for j in range(2)]
        B_vtm2 = [[Buf(f"vtm{j}{i}") for i in range(4)] for j in range(2)]
        ba = R.alloc("ba", [128, 4, 8], F32)
        beta2 = [R.alloc(f"beta{i}", [128, 4, 4], F32) for i in range(2)]
        gg2 = [R.alloc(f"gg{i}", [128, 4, 4], F32) for i in range(2)]
        gtmp = R.alloc("gtmp", [128, 4, 4], F32)
        B_ba = Buf("ba")
        B_beta2 = [Buf("beta0"), Buf("beta1")]
        B_gg2 = [Buf("gg0"), Buf("gg1")]
        B_gt = Buf("gtmp")
        G2 = R.alloc("G2", [128, 4, 512], F32)
        B_G2 = [Buf(f"G2{i}") for i in range(4)]
        zs, B_zs = rs, B_rs

        def two(name, shape, dt):
            return [R.alloc(f"{name}{i}", shape, dt) for i in range(2)], [Buf(f"{name}{i}") for i in range(2)]

        def one(name, shape, dt):
            return R.alloc(name, shape, dt), Buf(name)
        gcl4 = [R.alloc(f"gcl4_{i}", [128, 4, 12], F32) for i in range(2)]
        eg4 = [R.alloc(f"eg4_{i}", [128, 4, 12], F32) for i in range(2)]
        B_gcl4 = [Buf("gcl4_0"), Buf("gcl4_1")]
        B_eg4 = [Buf("eg4_0"), Buf("eg4_1")]
        gcT, B_gcT = two("gcT", [128, 128], F32)
        gcTb, B_gcTb = two("gcTb", [128, 512], F32)
        DT, B_DT = two("DT", [128, 512], F32)
        DE, B_DE = two("DE", [128, 512], F32)
        N0, B_N0 = two("N0", [128, 512], BF16)
        N0T, B_N0T = two("N0T", [128, 512], BF16)
        Na, B_Na = two("Na", [128, 512], BF16)
        NaT, B_NaT = two("NaT", [128, 512], BF16)
        Nb, B_Nb = two("Nb", [128, 512], BF16)
        NbT, B_NbT = two("NbT", [128, 512], BF16)
        Pm, B_Pm = two("Pm", [128, 512], BF16)
        kg, B_kg = two("kg", [128, 512], BF16)
        AI, B_AI = two("AI", [128, 512], BF16)
        qg, B_qg = two("qg", [128, 512], BF16)
        kdec, B_kdec = two("kdec", [128, 512], BF16)
        wT, B_wT = two("wT", [128, 512], BF16)
        ub, B_ub = two("ub", [128, 512], F32)
        vtmp, B_vtmp = one("vtmp", [128, 512], F32)
        vnew, B_vnew = one("vnew", [128, 512], BF16)
        S_f, B_Sf = one("S_f", [128, 512], F32)
        S_d, B_Sd = one("S_d", [128, 512], F32)
        S_b, B_Sb = one("S_b", [128, 512], BF16)
        osq, B_osq = vtmp, B_vtmp
        ssq, B_ssq = one("ssq", [128, 4], F32)
        rstd, B_rstd = one("rstd", [128, 4], F32)
        G2r, B_G2r = vtmp, B_vtmp
        og, B_og = two("og", [128, 512], BF16)

        PJ_BANK = (1, 0)
        PB = ((2, 3), (4, 5))
        BR, BO = 6, 7

        def v4(ap):
            return ap.rearrange("p (h f) -> p h f", h=4)

        def bc4(col_ap):
            return col_ap.unsqueeze(2).to_broadcast([128, 4, 128])

        P.op("pool", lambda e: e.memset(S_f[:], 0.0), writes=[B_Sf])
        P.op("pool", lambda e: e.memset(S_b[:], 0.0), writes=[B_Sb])
        P.op("pool", lambda e: e.memset(carry[:], 0.0), writes=B_carry)

        def chunk_chain(ci, xTs, BxT, slot):
            part, h = ci // 4, ci % 4
            pj, Bpj = bank(PJ_BANK[slot]), B_bk[PJ_BANK[slot]]
            for kc in range(8):
                P.op("pe", lambda e, kc=kc: e.matmul(pj, lhsT=Wd[:, kc, ci * 128:ci * 128 + 128], rhs=xTs[:, kc, 0:512], start=(kc == 0), stop=(kc == 7)),
                     reads=[B_Wd[part], BxT], writes=[Bpj])
                if kc % 2 == 1 and kc < 7:
                    yield
            rw, Brw = raw[slot], B_raw[slot]
            ca, Bca = cacc[slot], B_cacc[slot]
            P.op("pool", lambda e: e.tensor_copy(out=rw[:, 0:3], in_=carry[:, ci, :]), reads=[B_carry[ci]], writes=[Brw])
            yield
            P.op("act", lambda e: e.copy(out=rw[:, 3:515], in_=pj), reads=[Bpj], writes=[Brw])
            P.op("act", lambda e: e.activation(out=ca[:], in_=pj, func=AF.Identity, scale=cw_s[:, ci, 3:4]), reads=[Bpj, B_cst], writes=[Bca])
            yield
            P.op("pool", lambda e: e.tensor_copy(out=carry[:, ci, :], in_=rw[:, 512:515]), reads=[Brw], writes=[B_carry[ci]])
            for tap in range(3):
                P.op("dve", lambda e, tap=tap: e.scalar_tensor_tensor(
                    out=ca[:], in0=rw[:, tap:tap + 512], scalar=cw_s[:, ci, tap:tap + 1], in1=ca[:], op0=ALU.mult, op1=ALU.add),
                    reads=[Brw, Bca, B_cst], writes=[Bca])
                yield
            if part < 2:
                P.op("act", lambda e: e.activation(out=s_qk[:, ci, :], in_=ca[:], func=AF.Silu), reads=[Bca], writes=[B_sqk[ci]])
            else:
                P.op("act", lambda e: e.activation(out=vT[:, h, :], in_=ca[:], func=AF.Silu), reads=[Bca], writes=[B_vT[h]])
            yield

        def norm_chain(qi, sp):
            qT, kT, B_qT, B_kT = qT2[sp], kT2[sp], B_qT2[sp], B_kT2[sp]
            part, h = qi // 4, qi % 4
            sb_, Bsb = sq_b[qi % 2], B_sqb[qi % 2]
            r_, Br = rs[qi % 2], B_rs[qi % 2]
            bk_ = PJ_BANK[qi % 2]
            P.op("act", lambda e: e.activation(out=sb_[:], in_=s_qk[:, qi, :], func=AF.Square), reads=[B_sqk[qi]], writes=[Bsb])
            yield
            P.op("pe", lambda e: e.matmul(bank(bk_), lhsT=ones_b[:], rhs=sb_[:], start=True, stop=True), reads=[Bsb, B_cstb], writes=[B_bk[bk_]])
            yield
            P.op("act", lambda e: e.activation(out=r_[:], in_=bank(bk_), func=AF.Ln, bias=RMS_EPS), reads=[B_bk[bk_]], writes=[Br])
            P.op("act", lambda e: e.activation(out=r_[:], in_=r_[:], func=AF.Exp, scale=-0.5), reads=[Br], writes=[Br])
            yield
            if part == 0:
                P.op("dve", lambda e: e.scalar_tensor_tensor(out=qT[:, h, :], in0=s_qk[:, qi, :], scalar=128.0 ** -0.5, in1=r_[:],
                                                             op0=ALU.mult, op1=ALU.mult), reads=[B_sqk[qi], Br], writes=[B_qT[h]])
            else:
                P.op("dve", lambda e: e.tensor_tensor(out=kT[:, h, :], in0=s_qk[:, qi, :], in1=r_[:], op=ALU.mult),
                     reads=[B_sqk[qi], Br], writes=[B_kT[h]])
            yield

        def pre_chain(t, own, c, sp, can_tail):
            par = c
            qT, kT, B_qT, B_kT = qT2[sp], kT2[sp], B_qT2[sp], B_kT2[sp]
            k_tm, v_tm, B_ktm, B_vtm = k_tm2[sp], v_tm2[sp], B_ktm2[sp], B_vtm2[sp]
            beta, B_beta = beta2[sp], B_beta2[sp]
            tc0 = t * 128
            gl_, Bgl = gcl4[sp][:, t, :], B_gcl4[sp]
            eg_, Beg = eg4[sp][:, t, :], B_eg4[sp]
            bx, by = PB[c]
            X_, Y_, BX, BY = bank(bx), bank(by), B_bk[bx], B_bk[by]
            gcT_, BgT, gcTb_, BgTb = gcT[c], B_gcT[c], gcTb[c], B_gcTb[c]
            DT_, BDT, DE_, BDE = DT[c], B_DT[c], DE[c], B_DE[c]
            N0_, BN0, N0T_, BN0T, Pm_, BPm, kg_, Bkg = N0[c], B_N0[c], N0T[c], B_N0T[c], Pm[c], B_Pm[c], kg[c], B_kg[c]
            P.op("pe", lambda e: e.transpose(Y_[0:4, 0:128], gl_[:, 0:4], ident_f[:]), reads=[Bgl, B_cst], writes=[BY])
            for h in range(4):
                kTh = kT[:, h, tc0:tc0 + 128]
                P.op("pe", lambda e, kTh=kTh, h=h: e.matmul(X_[:, h * 128:(h + 1) * 128], lhsT=kTh, rhs=kTh, start=True, stop=True), reads=[B_kT[h]], writes=[BX])
            yield
            P.op("act", lambda e: e.copy(out=gcT_[0:4, :], in_=Y_[0:4, 0:128]), reads=[BY], writes=[BgT])
            P.op("dve", lambda e: e.tensor_tensor(out=gcTb_[0:4, :].rearrange("p (h f) -> p h f", h=4), in0=Y_[0:4, 0:128].unsqueeze(1).to_broadcast([4, 4, 128]),
                                                  in1=blk.rearrange("p (h f) -> p h f", h=4), op=ALU.mult), reads=[BY, B_c1b], writes=[BgTb])
            yield
            P.op("pe", lambda e: e.matmul(Y_, lhsT=ones_f[0:4, :], rhs=gcTb_[0:4, :], start=True, stop=False), reads=[BgTb, B_cst], writes=[BY])
            P.op("pe", lambda e: e.matmul(Y_, lhsT=gcT_[0:4, :], rhs=negsel, start=False, stop=False), reads=[BgT, B_c1b], writes=[BY])
            P.op("pe", lambda e: e.matmul(Y_, lhsT=ident_b[:], rhs=MD4_b[:], start=False, stop=True), reads=[B_cstb, B_c1a], writes=[BY])
            yield
            P.op("act", lambda e: e.activation(out=DT_[:], in_=Y_, func=AF.Exp), reads=[BY], writes=[BDT])
            yield
            P.op("dve", lambda e: e.tensor_tensor(out=v4(DE_[:]), in0=v4(DT_[:]), in1=bc4(beta[:, t, :]), op=ALU.mult), reads=[BDT, B_beta], writes=[BDE])
            P.op("dve", lambda e: e.tensor_tensor(out=N0_[:], in0=X_, in1=DE_[:], op=ALU.mult), reads=[BX, BDE], writes=[BN0])
            yield
            pstb = Y_.bitcast(BF16)
            for h in range(4):
                P.op("pe", lambda e, h=h: e.transpose(pstb[:, h * 128:(h + 1) * 128], N0_[:, h * 128:(h + 1) * 128], ident_b[:]), reads=[BN0, B_cstb], writes=[BY])
            P.op("dve", lambda e: e.tensor_tensor(out=v4(Pm_[:]), in0=ident_b[:].unsqueeze(1).to_broadcast([128, 4, 128]), in1=v4(N0_[:]), op=ALU.subtract), reads=[BN0, B_cstb], writes=[BPm])
            P.op("dve", lambda e: e.tensor_tensor(out=v4(kg_[:]), in0=v4(k_tm[:, t, :]), in1=bc4(eg_[:, 0:4]), op=ALU.mult), reads=[B_ktm[t], Beg], writes=[Bkg])
            yield
            P.op("act", lambda e: e.copy(out=N0T_[:], in_=pstb[:, 0:512]), reads=[BY], writes=[BN0T])
            yield
            cur, curT, Bc, BcT = N0_, N0T_, BN0, BN0T
            pp = ((Na[c], NaT[c], B_Na[c], B_NaT[c]), (Nb[c], NbT[c], B_Nb[c], B_NbT[c]))
            tb, mb, Btb, Bmb = Y_, X_, BY, BX
            for lv in range(1, 7):
                nxt, nxtT, Bn, BnT = pp[lv % 2]
                for h in range(4):
                    hs = slice(h * 128, (h + 1) * 128)
                    P.op("pe", lambda e, cur=cur, curT=curT, hs=hs, tb=tb: e.matmul(tb[:, hs], lhsT=cur[:, hs], rhs=curT[:, hs], start=True, stop=True), reads=[Bc, BcT], writes=[Btb])
                if lv < 6:
                    for h in range(4):
                        hs = slice(h * 128, (h + 1) * 128)
                        P.op("pe", lambda e, cur=cur, curT=curT, hs=hs, mb=mb: e.matmul(mb[:, hs], lhsT=curT[:, hs], rhs=cur[:, hs], start=True, stop=True), reads=[Bc, BcT], writes=[Bmb])
                yield
                P.op("act", lambda e, nxtT=nxtT, tb=tb: e.copy(out=nxtT[:], in_=tb), reads=[Btb], writes=[BnT])
                if lv < 6:
                    P.op("dve", lambda e, nxt=nxt, mb=mb: e.tensor_copy(out=nxt[:], in_=mb), reads=[Bmb], writes=[Bn])
                yield
                for h in range(4):
                    hs = slice(h * 128, (h + 1) * 128)
                    P.op("pe", lambda e, nxtT=nxtT, hs=hs, tb=tb: e.matmul(tb[:, hs], lhsT=nxtT[:, hs], rhs=Pm_[:, hs], start=True, stop=True), reads=[BnT, BPm], writes=[Btb])
                yield
                P.op("dve", lambda e, tb=tb: e.tensor_tensor(out=Pm_[:], in0=tb, in1=Pm_[:], op=ALU.add), reads=[Btb, BPm], writes=[BPm])
                yield
                cur, curT, Bc, BcT = nxt, nxtT, Bn, BnT
                tb, mb, Btb, Bmb = mb, tb, Bmb, Btb
            while not can_tail():
                yield
            for h in range(4):
                hs = slice(h * 128, (h + 1) * 128)
                P.op("pe", lambda e, hs=hs, tb=tb: e.matmul(tb[:, hs], lhsT=Pm_[:, hs], rhs=v_tm[:, t, hs], start=True, stop=True), reads=[BPm, B_vtm[t]], writes=[Btb])
            for h in range(4):
                hs = slice(h * 128, (h + 1) * 128)
                P.op("pe", lambda e, hs=hs, mb=mb: e.matmul(mb[:, hs], lhsT=kg_[:, hs], rhs=Pm_[:, hs], start=True, stop=True), reads=[Bkg, BPm], writes=[Bmb])
            P.op("dve", lambda e: e.tensor_tensor(out=v4(kdec[par][:]), in0=v4(k_tm[:, t, :]), in1=bc4(eg_[:, 4:8]), op=ALU.mult), reads=[B_ktm[t], Beg], writes=[B_kdec[par]])
            yield
            P.op("dve", lambda e, tb=tb: e.tensor_tensor(out=v4(ub[par][:]), in0=v4(tb), in1=bc4(beta[:, t, :]), op=ALU.mult), reads=[Btb, B_beta], writes=[B_ub[par]])
            P.op("act", lambda e, mb=mb: e.copy(out=wT[par][:], in_=mb), reads=[Bmb], writes=[B_wT[par]])
            yield
            if own:
                for h in range(4):
                    kTh = kT[:, h, tc0:tc0 + 128]
                    qTh = qT[:, h, tc0:tc0 + 128]
                    P.op("pe", lambda e, kTh=kTh, qTh=qTh, h=h, tb=tb: e.matmul(tb[:, h * 128:(h + 1) * 128], lhsT=kTh, rhs=qTh, start=True, stop=True), reads=[B_kT[h], B_qT[h]], writes=[Btb])
                P.op("pe", lambda e, mb=mb: e.matmul(mb, lhsT=ones_f[0:4, :], rhs=gcTb_[0:4, :], start=True, stop=True), reads=[BgTb, B_cst], writes=[Bmb])
                P.op("dve", lambda e: e.tensor_tensor(out=v4(DT_[:]), in0=v4(DT_[:]), in1=ident_f[:].unsqueeze(1).to_broadcast([128, 4, 128]), op=ALU.add),
                     reads=[BDT, B_cst], writes=[BDT])
                yield
                P.op("dve", lambda e, tb=tb: e.tensor_tensor(out=AI[par][:], in0=tb, in1=DT_[:], op=ALU.mult), reads=[Btb, BDT], writes=[B_AI[par]])
                P.op("act", lambda e, mb=mb: e.activation(out=DE_[:], in_=mb, func=AF.Exp), reads=[Bmb], writes=[BDE])
                yield
                P.op("dve", lambda e: e.tensor_tensor(out=v4(qg[par][:]), in0=qT[:, :, tc0:tc0 + 128], in1=v4(DE_[:]), op=ALU.mult), reads=B_qT + [BDE], writes=[B_qg[par]])
                yield

        def rec_chain(t, own, par, gtile, sp):
            beta, B_beta = beta2[sp], B_beta2[sp]
            R_, O_ = bank(BR), bank(BO)
            BBR, BBO = B_bk[BR], B_bk[BO]
            eg_, Beg = eg4[sp][:, t, :], B_eg4[sp]
            P.op("dve", lambda e: e.tensor_tensor(out=v4(S_d[:]), in0=v4(S_f[:]), in1=bc4(eg_[:, 8:12]), op=ALU.mult), reads=[B_Sf, Beg], writes=[B_Sd])
            for h in range(4):
                hs = slice(h * 128, (h + 1) * 128)
                P.op("pe", lambda e, hs=hs: e.matmul(R_[:, hs], lhsT=wT[par][:, hs], rhs=S_b[:, hs], start=True, stop=True), reads=[B_wT[par], B_Sb], writes=[BBR])
            if own:
                for h in range(4):
                    hs = slice(h * 128, (h + 1) * 128)
                    P.op("pe", lambda e, hs=hs, h=h: e.matmul(O_[:, hs], lhsT=qg[par][:, hs], rhs=S_b[:, hs], start=(h == 0), stop=False, skip_group_check=True), reads=[B_qg[par], B_Sb], writes=[BBO])
            yield
            P.op("dve", lambda e: e.tensor_tensor(out=v4(vtmp[:]), in0=v4(R_), in1=bc4(beta[:, t, :]), op=ALU.mult), reads=[BBR, B_beta], writes=[B_vtmp])
            P.op("dve", lambda e: e.tensor_tensor(out=vnew[:], in0=ub[par][:], in1=vtmp[:], op=ALU.subtract), reads=[B_ub[par], B_vtmp], writes=[B_vnew])
            yield
            for h in range(4):
                hs = slice(h * 128, (h + 1) * 128)
                P.op("pe", lambda e, hs=hs: e.matmul(R_[:, hs], lhsT=kdec[par][:, hs], rhs=vnew[:, hs], start=True, stop=True), reads=[B_kdec[par], B_vnew], writes=[BBR])
            yield
            P.op("dve", lambda e: e.tensor_tensor(out=S_f[:], in0=R_, in1=S_d[:], op=ALU.add), reads=[BBR, B_Sd], writes=[B_Sf])
            if own:
                for h in range(4):
                    hs = slice(h * 128, (h + 1) * 128)
                    P.op("pe", lambda e, hs=hs, h=h: e.matmul(O_[:, hs], lhsT=AI[par][:, hs], rhs=vnew[:, hs], start=False, stop=(h == 3), skip_group_check=True), reads=[B_AI[par], B_vnew], writes=[BBO])
            yield
            P.op("act", lambda e: e.copy(out=S_b[:], in_=S_f[:]), reads=[B_Sf], writes=[B_Sb])
            if not own:
                return
            yield
            P.op("act", lambda e: e.activation(out=osq[:], in_=O_, func=AF.Square), reads=[BBO], writes=[B_osq])
            yield
            P.op("dve", lambda e: e.tensor_reduce(out=ssq[:], in_=v4(osq[:]), axis=mybir.AxisListType.X, op=ALU.add), reads=[B_osq], writes=[B_ssq])
            P.op("dve", lambda e: e.tensor_scalar(out=rstd[:], in0=ssq[:], scalar1=1.0 / 128.0, scalar2=RMS_EPS, op0=ALU.mult, op1=ALU.add), reads=[B_ssq], writes=[B_rstd])
            yield
            P.op("act", lambda e: e.activation(out=rstd[:], in_=rstd[:], func=AF.Ln), reads=[B_rstd], writes=[B_rstd])
            P.op("act", lambda e: e.activation(out=rstd[:], in_=rstd[:], func=AF.Exp, scale=-0.5), reads=[B_rstd], writes=[B_rstd])
            yield
            P.op("dve", lambda e: e.tensor_tensor(out=v4(G2r[:]), in0=v4(G2[:, t, :]), in1=bc4(rstd[:]), op=ALU.mult), reads=[B_G2[t], B_rstd], writes=[B_G2r])
            yield
            o_, Bo = og[t % 2], B_og[t % 2]
            P.op("dve", lambda e: e.tensor_tensor(out=o_[:], in0=O_, in1=G2r[:], op=ALU.mult), reads=[BBO, B_G2r], writes=[Bo])
            yield
            pst = bank_bf(BR)
            for h in range(4):
                P.op("pe", lambda e, h=h: e.transpose(pst[:, h * 128:(h + 1) * 128], o_[:, h * 128:(h + 1) * 128], ident_b[:]), reads=[Bo, B_cstb], writes=[BBR])
            yield
            P.op("act", lambda e: e.copy(out=obT[:, :, gtile * 128:(gtile + 1) * 128], in_=pst[:, 0:512].rearrange("p (h t) -> p h t", h=4)),
                 reads=[BBR], writes=[B_obT[gtile]])
            yield

        def par_(gens):
            active = list(gens)
            while active:
                for g_ in list(active):
                    try:
                        next(g_)
                    except StopIteration:
                        active.remove(g_)
                yield

        def stage_gen(st):
            own = st >= 4
            sp = st % 2
            xTs, BxT = xT[sp], B_xT[sp]
            qT, kT, B_qT, B_kT = qT2[sp], kT2[sp], B_qT2[sp], B_kT2[sp]
            k_tm, v_tm, B_ktm, B_vtm = k_tm2[sp], v_tm2[sp], B_ktm2[sp], B_vtm2[sp]
            beta, gg, B_beta, B_gg = beta2[sp], gg2[sp], B_beta2[sp], B_gg2[sp]
            for t in range(4):
                if own:
                    rows = x_d[(st - 4) * 512 + t * 128:(st - 4) * 512 + (t + 1) * 128, :]
                else:
                    rows = xp_d[st * 512 + t * 128:st * 512 + (t + 1) * 128, :]
                xt.tile(rows, xTs, (t * 128, (t + 1) * 128), BxT, evac=("act" if t % 2 == 0 else "dve"))
                if first_stage[0] and t == 1:
                    early_loads()
                if first_stage[0] and t == 3:
                    late_loads()
                    first_stage[0] = False
                yield
            for t in range(4):
                proj_tm(bank(0)[:, t * 8:(t + 1) * 8], B_bk[0], Wba, B_Wba, 0, 8, xTs, BxT, t * 128)
                yield
            P.op("dve", lambda e: e.tensor_copy(out=ba[:], in_=bank(0)[:, 0:32].rearrange("p (t c) -> p t c", t=4)), reads=[B_bk[0]], writes=[B_ba])
            yield
            P.op("act", lambda e: e.activation(out=beta[:], in_=ba[:, :, 0:4], func=AF.Tanh, scale=0.5), reads=[B_ba], writes=[B_beta])
            P.op("dve", lambda e: e.tensor_tensor(out=gtmp[:], in0=ba[:, :, 4:8], in1=small_s[:, 12:16].unsqueeze(1).to_broadcast([128, 4, 4]), op=ALU.add),
                 reads=[B_ba, B_cst], writes=[B_gt])
            yield
            P.op("dve", lambda e: e.tensor_scalar(out=beta[:], in0=beta[:], scalar1=0.5, scalar2=0.5, op0=ALU.mult, op1=ALU.add), reads=[B_beta], writes=[B_beta])
            order = [4, 5, 6, 7, 0, 1, 2, 3, 8, 9, 10, 11]
            for i0 in range(0, 12, NR):
                yield from par_([chunk_chain(order[i0 + j], xTs, BxT, j) for j in range(NR)])
            P.op("act", lambda e: e.activation(out=gtmp[:], in_=gtmp[:], func=AF.Exp), reads=[B_gt], writes=[B_gt])
            P.op("act", lambda e: e.activation(out=gtmp[:], in_=gtmp[:], func=AF.Ln, bias=1.0), reads=[B_gt], writes=[B_gt])
            yield
            P.op("dve", lambda e: e.tensor_tensor(out=gg[:], in0=gtmp[:], in1=nega_s[:].unsqueeze(1).to_broadcast([128, 4, 4]), op=ALU.mult),
                 reads=[B_gt, B_der], writes=[B_gg])
            yield
            gl4, Bg4, e4, Be4 = gcl4[sp], B_gcl4[sp], eg4[sp], B_eg4[sp]
            ggf = gg[:].rearrange("p t h -> p (t h)")
            P.op("pe", lambda e: e.matmul(bank(0)[:, 0:16], lhsT=U_f[:], rhs=ggf, start=True, stop=True), reads=[B_gg, B_cst], writes=[B_bk[0]])
            P.op("pe", lambda e: e.matmul(bank(0)[:, 16:32], lhsT=ones_f[:], rhs=ggf, start=True, stop=True), reads=[B_gg, B_cst], writes=[B_bk[0]])
            yield
            P.op("dve", lambda e: e.tensor_copy(out=gl4[:, :, 0:4], in_=bank(0)[:, 0:16].rearrange("p (t h) -> p t h", t=4)), reads=[B_bk[0]], writes=[Bg4])
            P.op("dve", lambda e: e.tensor_copy(out=gl4[:, :, 8:12], in_=bank(0)[:, 16:32].rearrange("p (t h) -> p t h", t=4)), reads=[B_bk[0]], writes=[Bg4])
            P.op("dve", lambda e: e.tensor_tensor(out=gl4[:, :, 4:8], in0=gl4[:, :, 8:12], in1=gl4[:, :, 0:4], op=ALU.subtract), reads=[Bg4], writes=[Bg4])
            yield
            P.op("act", lambda e: e.activation(out=e4[:], in_=gl4[:], func=AF.Exp), reads=[Bg4], writes=[Be4])
            yield
            for i0 in range(0, 8, 2):
                yield from par_([norm_chain(qi, sp) for qi in ((4 + i0, 5 + i0) if i0 < 4 else (i0 - 4, i0 - 3))])
            for t in range(4):
                for (srcT, Bsrc, dst, Bdst, evac) in ((kT, B_kT, k_tm, B_ktm, "act"), (vT, B_vT, v_tm, B_vtm, "dve")):
                    pst = bank_bf(0)
                    for h in range(4):
                        P.op("pe", lambda e, h=h, srcT=srcT, pst=pst, t=t: e.transpose(pst[:, h * 128:(h + 1) * 128], srcT[:, h, t * 128:(t + 1) * 128], ident_b[:]),
                             reads=[Bsrc[h], B_cstb], writes=[B_bk[0]])
                    yield
                    if evac == "act":
                        P.op("act", lambda e, dst=dst, pst=pst, t=t: e.copy(out=dst[:, t, :], in_=pst[:, 0:512]), reads=[B_bk[0]], writes=[Bdst[t]])
                    else:
                        P.op("dve", lambda e, dst=dst, pst=pst, t=t: e.tensor_copy(out=dst[:, t, :], in_=pst[:, 0:512]), reads=[B_bk[0]], writes=[Bdst[t]])
                    yield

        def dz_part(st):
            sp = st % 2
            xTs, BxT = xT[sp], B_xT[sp]
            for t in range(4):
                bk_ = PJ_BANK[t % 2]
                proj_tm(bank(bk_), B_bk[bk_], Wz, B_Wz, 0, 512, xTs, BxT, t * 128)
                z, Bz = zs[t % 2], B_zs[t % 2]
                P.op("act", lambda e, z=z, bk_=bk_: e.activation(out=z[:], in_=bank(bk_), func=AF.Silu), reads=[B_bk[bk_]], writes=[Bz])
                P.op("dve", lambda e, z=z, t=t: e.tensor_tensor(out=v4(G2[:, t, :]), in0=v4(z[:]), in1=nw_s[:].unsqueeze(1).to_broadcast([128, 4, 128]), op=ALU.mult),
                     reads=[Bz, B_cst], writes=[B_G2[t]])

        def tiles_gen(st):
            own = st >= 4
            sp = st % 2
            pre_done = [False] * 4
            rec_done = [False] * 4

            def seq_pre(c, lag):
                for _ in range(lag):
                    yield
                for t in (c, c + 2):
                    yield from pre_chain(t, own, c, sp, (lambda t=t: t < 2 or rec_done[t - 2]))
                    pre_done[t] = True

            def seq_rec():
                for t in range(4):
                    while not pre_done[t]:
                        yield
                    gtile = (st - 4) * 4 + t if own else None
                    yield from rec_chain(t, own, t % 2, gtile, sp)
                    rec_done[t] = True

            yield from par_([seq_pre(0, 0), seq_pre(1, 24), seq_rec()])

        st_list = list(range(8))
        run_chains([stage_gen(st_list[0])])
        for i_, st in enumerate(st_list):
            if st >= 4:
                dz_part(st)
            tg_ = tiles_gen(st)
            sg_ = stage_gen(st_list[i_ + 1]) if i_ + 1 < len(st_list) else None
            W_T = 1
            t_done = s_done = False
            while not (t_done and (s_done or sg_ is None)):
                for _ in range(W_T):
                    if not t_done:
                        try:
                            next(tg_)
                        except StopIteration:
                            t_done = True
                if sg_ is not None and not s_done:
                    try:
                        next(sg_)
                    except StopIteration:
                        s_done = True

    def phase_1b():
        R = Region(nc, BASE + 124 * KB, TOP)
        Wq = R.alloc("Wq", [128, 8, 512], BF16)
        Wk = R.alloc("Wk", [128, 8, 256], BF16)
        Wv = R.alloc("Wv", [128, 8, 128], BF16)
        B_Wq, B_Wk, B_Wv = Buf("Wq"), Buf("Wk"), Buf("Wv")
        lw = [P.lane(f"w1b{i}") for i in range(3)]
        def emit_k(e):
            r = []
            for g in range(2):
                for d in range(2):
                    r.append(e.dma_start(out=Wk[:, :, g * 128 + d * 64:g * 128 + (d + 1) * 64], in_=w_in_v[:, :, C_AK + g * 64:C_AK + (g + 1) * 64]))
            return r

        B_b0 = Buf("b0", excl=True)
        xt = XT(R, 0, B_b0)
        xT = [R.alloc(f"xT{i}", [128, 8, 512], BF16) for i in range(2)]
        B_xT = [Buf(f"xT{i}") for i in range(2)]
        xTh = R.alloc("xTh", [128, 8, 128], BF16)
        B_xTh = Buf("xTh")
        aqT = R.alloc("aqT", [128, 4, 512], BF16)
        B_aqT = [Buf(f"aqT{i}") for i in range(4)]
        kkT = R.alloc("kkT", [128, 2, 640], BF16)
        B_kkT = [Buf(f"kkT{i}") for i in range(2)]
        Vaug = R.alloc("Vaug", [128, 5, 2, 80], BF16)
        B_V = [Buf(f"V{i}") for i in range(5)]
        eS = [R.alloc(f"eS{i}", [128, 2, 512], BF16) for i in range(2)]
        B_eS = [Buf(f"eS{i}") for i in range(2)]
        den = R.alloc("den", [128, 2, 4], F32)
        B_den = [Buf("den0"), Buf("den1")]
        oa = [R.alloc(f"oa{i}", [128, 512], BF16) for i in range(2)]
        B_oa = [Buf(f"oa{i}") for i in range(2)]
        B_pj = [Buf("pj0", excl=True), Buf("pj1", excl=True)]
        B_b3 = Buf("b3", excl=True)
        B_S = [Buf("S0", excl=True), Buf("S1", excl=True)]
        B_b2 = Buf("b2", excl=True)
        B_o = [Buf("o0", excl=True), B_b2]

        P.op("pool", lambda e: e.memset(Vaug[:], 1.0), writes=B_V)

        def proj_k_v(xTs, BxT, x0, n, kk_c0, vslot0):
            for g in range(2):
                pj, Bpj = bank(1)[:, 0:n], B_pj[0]
                proj_fm(pj, Bpj, Wk, B_Wk, g * 128, xTs, BxT, x0, n)
                P.op("act", lambda e, g=g, pj=pj: e.copy(out=kkT[:, g, kk_c0:kk_c0 + n], in_=pj), reads=[Bpj], writes=[B_kkT[g]])
            for t in range(n // 128):
                proj_tm(bank(2)[:, 0:128], B_b2, Wv, B_Wv, 0, 128, xTs, BxT, x0 + t * 128)
                P.op("dve", lambda e, t=t: e.tensor_copy(out=Vaug[:, vslot0 + t, :, 0:64], in_=bank(2)[:, 0:128].rearrange("p (g d) -> p g d", g=2)),
                     reads=[B_b2], writes=[B_V[vslot0 + t]])

        xt.tile(xp_d[T_CORE - 128:T_CORE, :], xTh, (0, 128), B_xTh)
        P.op("pool", emit_k, writes=[B_Wk], lane=lw[1], ndma=4)
        load_w(Wv, B_Wv, lw[2], (C_AV, C_AV + 128))
        load_w(Wq, B_Wq, lw[0], (C_AQ, C_AQ + 512))
        proj_k_v(xTh, B_xTh, 0, 128, 0, 0)

        for st in range(4):
            xTs, BxT = xT[st % 2], B_xT[st % 2]
            for t in range(4):
                xt.tile(x_d[st * 512 + t * 128:st * 512 + (t + 1) * 128, :], xTs, (t * 128, (t + 1) * 128), BxT, evac=("act" if t % 2 == 0 else "dve"))
            if st == 1:
                prefetch_1c()
            if st > 0:
                for g in range(2):
                    P.op("pool", lambda e, g=g: e.tensor_copy(out=kkT[:, g, 0:128], in_=kkT[:, g, 512:640]), reads=[B_kkT[g]], writes=[B_kkT[g]])
                P.op("pool", lambda e: e.tensor_copy(out=Vaug[:, 0, :, 0:64], in_=Vaug[:, 4, :, 0:64]), reads=[B_V[4]], writes=[B_V[0]])
            proj_k_v(xTs, BxT, 0, 512, 128, 1)
            for c in range(4):
                pj, Bpj = bank(1), B_pj[0]
                proj_fm(pj, Bpj, Wq, B_Wq, c * 128, xTs, BxT, 0, 512)
                P.op("act", lambda e, c=c, pj=pj: e.copy(out=aqT[:, c, :], in_=pj), reads=[Bpj], writes=[B_aqT[c]])
            def grp_chain(t, g, gtile, o_, Bo):
                Sps = PS[2 + g]
                for kb in range(2):
                    mask = (Mprev0_b if gtile == 0 else Mprev_b) if kb == 0 else Mown_b
                    kc0 = t * 128 + kb * 128
                    for hh in range(2):
                        sp_ = Sps[:, hh * 512 + kb * 256:hh * 512 + (kb + 1) * 256]
                        P.op("pe", lambda e, sp_=sp_, mask=mask: e.matmul(sp_, lhsT=ident_b[:], rhs=mask[:, 0:256], start=True, stop=False), reads=[B_cstb], writes=[B_S[g]])
                    for hh in range(2):
                        sp_ = Sps[:, hh * 512 + kb * 256:hh * 512 + (kb + 1) * 256]
                        P.op("pe", lambda e, sp_=sp_, hh=hh, kc0=kc0: e.matmul(
                            sp_.rearrange("p (c q) -> p c q", c=2),
                            lhsT=kkT[hh * 64:(hh + 1) * 64, g, kc0:kc0 + 128],
                            rhs=aqT[hh * 64:(hh + 1) * 64, 2 * g:2 * g + 2, t * 128:(t + 1) * 128],
                            start=False, stop=True), reads=[B_kkT[g], B_aqT[2 * g], B_aqT[2 * g + 1]], writes=[B_S[g]])
                yield
                for hh in range(2):
                    P.op("act", lambda e, hh=hh: e.activation(out=eS[g][:, hh, :], in_=Sps[:, hh * 512:(hh + 1) * 512], func=AF.Exp, scale=0.125), reads=[B_S[g]], writes=[B_eS[g]])
                yield
                ops_ = (bank(3)[:, 0:320] if g == 0 else bank(2)[:, 128:448]).rearrange("p (h d) -> p h d", h=4)
                for hp in range(4):
                    for kb in range(2):
                        P.op("pe", lambda e, hp=hp, kb=kb: e.matmul(ops_[:, hp, :], lhsT=eS[g][:, hp // 2, kb * 256 + (hp % 2) * 128:kb * 256 + (hp % 2 + 1) * 128],
                                                                    rhs=Vaug[:, t + kb, g, :], start=(kb == 0), stop=(kb == 1)),
                             reads=[B_eS[g], B_V[t + kb]], writes=[B_o[g]])
                yield
                P.op("dve", lambda e: e.tensor_tensor(out=den[:, g, :].rearrange("p (a b) -> p a b", a=2), in0=ops_[:, :, 64].rearrange("p (a b) -> p a b", a=2),
                                                      in1=esink_s[:, g * 4:(g + 1) * 4].rearrange("p (c a) -> p a c", c=2, a=2), op=ALU.add),
                     reads=[B_o[g], B_der], writes=[B_den[g]])
                P.op("dve", lambda e: e.reciprocal(out=den[:, g, :], in_=den[:, g, :]), reads=[B_den[g]], writes=[B_den[g]])
                for a_ in range(2):
                    P.op("dve", lambda e, a_=a_: e.tensor_tensor(
                        out=o_[:, g * 256:(g + 1) * 256].rearrange("p (c a d) -> p a c d", c=2, a=2, d=64)[:, a_],
                        in0=ops_[:, 2 * a_:2 * a_ + 2, 0:64],
                        in1=den[:, g, 2 * a_:2 * a_ + 2].unsqueeze(2).to_broadcast([128, 2, 64]), op=ALU.mult),
                        reads=[B_o[g], B_den[g]], writes=[Bo])
                yield

            def tail_chain(gtile, o_, Bo):
                pst = bank_bf(0)
                for c in range(4):
                    P.op("pe", lambda e, c=c: e.transpose(pst[:, c * 128:(c + 1) * 128], o_[:, c * 128:(c + 1) * 128], ident_b[:]), reads=[Bo, B_cstb], writes=[B_b0])
                yield
                P.op("act", lambda e: e.copy(out=oaT[:, :, gtile * 128:(gtile + 1) * 128], in_=pst[:, 0:512].rearrange("p (h t) -> p h t", h=4)),
                     reads=[B_b0], writes=[B_oaT[gtile]])
                yield

            done = [0] * 4
            tdone = [False] * 4

            def seq_grp(g, lag):
                for _ in range(lag):
                    yield
                for t in range(4):
                    while t >= 2 and not tdone[t - 2]:
                        yield
                    yield from grp_chain(t, g, st * 4 + t, oa[t % 2], B_oa[t % 2])
                    done[t] += 1

            def seq_tail():
                for t in range(4):
                    while done[t] < 2:
                        yield
                    yield from tail_chain(st * 4 + t, oa[t % 2], B_oa[t % 2])
                    tdone[t] = True

            run_chains([seq_grp(0, 0), seq_grp(1, 3), seq_tail()])

    PF = {}

    def prefetch_1c():
        R = Region(nc, BASE + 76 * KB, BASE + 124 * KB)
        Wg = R.alloc("Wg", [128, 8, 2048], BF16)
        Wao = R.alloc("Wao", [128, 4, D], BF16)
        Wdo = R.alloc("Wdo", [128, 4, D], BF16)
        B_Wg = [Buf(f"Wg{i}") for i in range(4)]
        B_Wao, B_Wdo = Buf("Wao"), Buf("Wdo")
        lw = [P.lane(f"w1c{i}") for i in range(6)]
        P.op("pool", lambda e: e.dma_start(out=Wao[:], in_=w_ao_d.rearrange("(kc p) c -> p kc c", p=128)), writes=[B_Wao], lane=lw[4])
        P.op("pool", lambda e: e.dma_start(out=Wdo[:], in_=w_do_d.rearrange("(kc p) c -> p kc c", p=128)), writes=[B_Wdo], lane=lw[5])
        for i in (0, 2, 1, 3):
            def emit(e, i=i):
                return [e.dma_start(out=Wg[:, 2 * j:2 * j + 2, i * 512:(i + 1) * 512], in_=w_in_v[:, 2 * j:2 * j + 2, C_G + i * 512:C_G + (i + 1) * 512]) for j in range(4)]
            P.op("pool", emit, writes=[B_Wg[i]], lane=lw[i], ndma=4)
        PF.update(Wg=Wg, Wao=Wao, Wdo=Wdo, B_Wg=B_Wg, B_Wao=B_Wao, B_Wdo=B_Wdo)

    def phase_1c():
        if not PF:
            prefetch_1c()
        R = Region(nc, BASE + 124 * KB, TOP)
        Wg, Wao, Wdo, B_Wg, B_Wao, B_Wdo = PF["Wg"], PF["Wao"], PF["Wdo"], PF["B_Wg"], PF["B_Wao"], PF["B_Wdo"]
        xt = XT(R, 0, Buf("b0", excl=True))
        xT = [R.alloc(f"xT{i}", [128, 8, 512], BF16) for i in range(2)]
        B_xT = [Buf(f"xT{i}") for i in range(2)]
        ta = [R.alloc(f"ta{i}", [128, 512], F32) for i in range(2)]
        tb = [R.alloc(f"tb{i}", [128, 512], F32) for i in range(2)]
        m1 = [R.alloc(f"m1{i}", [128, 512], F32) for i in range(2)]
        B_ta = [Buf(f"ta{i}") for i in range(2)]
        B_tb = [Buf(f"tb{i}") for i in range(2)]
        B_m1 = [Buf(f"m1{i}") for i in range(2)]
        B_bk = [Buf(f"bk{i}", excl=True) for i in range(8)]
        it = 0
        for st in range(4):
            xTs, BxT = xT[st % 2], B_xT[st % 2]
            for t in range(4):
                xt.tile(x_d[st * 512 + t * 128:st * 512 + (t + 1) * 128, :], xTs, (t * 128, (t + 1) * 128), BxT, evac=("act" if t % 2 == 0 else "dve"))
            if st == 1:
                prefetch_1d()
            tok0 = st * 512
            Bo_a = B_oaT[st * 4:st * 4 + 4]
            Bo_b = B_obT[st * 4:st * 4 + 4]
            for m in range(8):
                u_ = it % 2
                it += 1
                ia, ib, iya, iyb = (1, 2, 3, 4) if u_ == 0 else (5, 6, 7, 4)
                proj_fm(bank(ia), B_bk[ia], Wg, B_Wg[m // 4], m * 128, xTs, BxT, 0, 512)
                proj_fm(bank(ib), B_bk[ib], Wg, B_Wg[2 + m // 4], 1024 + m * 128, xTs, BxT, 0, 512)
                for kc in range(4):
                    P.op("pe", lambda e, kc=kc, iya=iya, m=m, tok0=tok0: e.matmul(bank(iya), lhsT=Wao[:, kc, m * 128:(m + 1) * 128], rhs=oaT[:, kc, tok0:tok0 + 512], start=(kc == 0), stop=(kc == 3)),
                         reads=[B_Wao] + Bo_a, writes=[B_bk[iya]])
                P.op("act", lambda e, u_=u_, ia=ia: e.activation(out=ta[u_][:], in_=bank(ia), func=AF.Tanh, scale=0.5), reads=[B_bk[ia]], writes=[B_ta[u_]])
                P.op("dve", lambda e, u_=u_, iya=iya: e.scalar_tensor_tensor(out=m1[u_][:], in0=ta[u_][:], scalar=1.0, in1=bank(iya), op0=ALU.add, op1=ALU.mult),
                     reads=[B_ta[u_], B_bk[iya]], writes=[B_m1[u_]])
                for kc in range(4):
                    P.op("pe", lambda e, kc=kc, iyb=iyb, m=m, tok0=tok0: e.matmul(bank(iyb), lhsT=Wdo[:, kc, m * 128:(m + 1) * 128], rhs=obT[:, kc, tok0:tok0 + 512], start=(kc == 0), stop=(kc == 3)),
                         reads=[B_Wdo] + Bo_b, writes=[B_bk[iyb]])
                P.op("act", lambda e, u_=u_, ib=ib: e.activation(out=tb[u_][:], in_=bank(ib), func=AF.Tanh, scale=0.5), reads=[B_bk[ib]], writes=[B_tb[u_]])
                P.op("dve", lambda e, u_=u_, iyb=iyb: e.scalar_tensor_tensor(out=tb[u_][:], in0=tb[u_][:], scalar=1.0, in1=bank(iyb), op0=ALU.add, op1=ALU.mult),
                     reads=[B_tb[u_], B_bk[iyb]], writes=[B_tb[u_]])
                P.op("dve", lambda e, u_=u_, m=m, tok0=tok0: e.tensor_tensor(out=mgT[:, m, tok0:tok0 + 512], in0=m1[u_][:], in1=tb[u_][:], op=ALU.add),
                     reads=[B_m1[u_], B_tb[u_]], writes=[B_mgT[st]])

    class LN:
        def __init__(self, R, name, ns=2):
            self.ns = ns
            self.stat = [R.alloc(f"{name}_st{i}", [128, 8], F32) for i in range(ns)]
            self.B_stat = [Buf(f"{name}_st{i}") for i in range(ns)]
            self.junk = R.alloc(f"{name}_junk", [128, D], BF16)
            self.B_junk = Buf(f"{name}_junk")
            self.n = 0

        def tile(self, src_ap, src_bufs, dst_ap, dst_bufs, g_t, b_t, B_lnp, eps_eff):
            i = self.n % self.ns
            self.n += 1
            stat, B_stat, junk, B_junk = self.stat[i], self.B_stat[i], self.junk, self.B_junk
            P.op("dve", lambda e: e.memset(stat[:], 0.0), writes=[B_stat])
            yield
            P.op("act", lambda e: e.activation(out=junk[:], in_=src_ap, func=AF.Identity, accum_out=stat[:, 0:1]), reads=src_bufs, writes=[B_junk, B_stat])
            P.op("act", lambda e: e.activation(out=junk[:], in_=src_ap, func=AF.Square, accum_out=stat[:, 1:2]), reads=src_bufs, writes=[B_junk, B_stat])
            yield
            P.op("dve", lambda e: e.tensor_scalar(out=stat[:, 2:3], in0=stat[:, 0:1], scalar1=1.0 / D, scalar2=None, op0=ALU.mult), reads=[B_stat], writes=[B_stat])
            P.op("dve", lambda e: e.tensor_tensor(out=stat[:, 3:4], in0=stat[:, 2:3], in1=stat[:, 2:3], op=ALU.mult), reads=[B_stat], writes=[B_stat])
            P.op("dve", lambda e: e.scalar_tensor_tensor(out=stat[:, 4:5], in0=stat[:, 1:2], scalar=1.0 / D, in1=stat[:, 3:4], op0=ALU.mult, op1=ALU.subtract), reads=[B_stat], writes=[B_stat])
            yield
            P.op("act", lambda e: e.activation(out=stat[:, 5:6], in_=stat[:, 4:5], func=AF.Ln, bias=eps_eff), reads=[B_stat], writes=[B_stat])
            P.op("act", lambda e: e.activation(out=stat[:, 5:6], in_=stat[:, 5:6], func=AF.Exp, scale=-0.5), reads=[B_stat], writes=[B_stat])
            yield
            P.op("dve", lambda e: e.scalar_tensor_tensor(out=stat[:, 6:7], in0=stat[:, 2:3], scalar=-1.0, in1=stat[:, 5:6], op0=ALU.mult, op1=ALU.mult), reads=[B_stat], writes=[B_stat])
            yield
            P.op("act", lambda e: e.activation(out=dst_ap, in_=src_ap, func=AF.Identity, bias=stat[:, 6:7], scale=stat[:, 5:6]), reads=src_bufs + [B_stat], writes=dst_bufs)
            yield
            P.op("dve", lambda e: e.tensor_tensor(out=dst_ap, in0=dst_ap, in1=g_t, op=ALU.mult), reads=dst_bufs + [B_lnp], writes=dst_bufs)
            yield
            P.op("dve", lambda e: e.tensor_tensor(out=dst_ap, in0=dst_ap, in1=b_t, op=ALU.add), reads=dst_bufs + [B_lnp], writes=dst_bufs)
            yield

    PF1D = {}

    def prefetch_1d():
        R = Region(nc, BASE + 172 * KB, BASE + 196 * KB)
        Wo = R.alloc("Wo", [128, 8, D], BF16)
        B_Wo = Buf("Wo")
        lwo = P.lane("wo")
        P.op("pool", lambda e: [e.dma_start(out=Wo[:, 2 * j:2 * j + 2, :], in_=w_out_d.rearrange("(kc p) c -> p kc c", p=128)[:, 2 * j:2 * j + 2, :]) for j in range(4)],
             writes=[B_Wo], lane=lwo, ndma=4)
        lnp = R.alloc("lnp1", [128, 2, D], F32)
        B_lnp = Buf("lnp1")
        llnp = P.lane("lnp1")
        P.op("sp", lambda e: e.dma_start(out=lnp[:].rearrange("p a d -> p (a d)"), in_=lnp_d[:, 0:2 * D]), writes=[B_lnp], lane=llnp)
        PF1D.update(Wo=Wo, B_Wo=B_Wo, lnp=lnp, B_lnp=B_lnp)

    def phase_1d():
        if not PF1D:
            prefetch_1d()
        Wo, B_Wo, lnp, B_lnp = PF1D["Wo"], PF1D["B_Wo"], PF1D["lnp"], PF1D["B_lnp"]
        NW = 4
        R = Region(nc, BASE + 12 * KB, BASE + 44 * KB)
        R3 = Region(nc, BASE + 196 * KB, TOP)
        xs = [R.alloc(f"xs{i}", [128, D], F32) for i in range(NW)]
        B_xs = [Buf(f"xs{i}") for i in range(NW)]
        L_xs = [P.lane(f"xs{i}") for i in range(NW)]
        pre = [R.alloc(f"pre{i}", [128, D], F32) for i in range(NW)]
        B_pre = [Buf(f"pre{i}") for i in range(NW)]
        x1b = [R3.alloc(f"x1b{i}", [128, D], BF16) for i in range(NW)]
        B_x1b = [Buf(f"x1b{i}") for i in range(NW)]
        ln = LN(R3, "ln1", ns=NW)
        B_mix = [Buf(f"mix{i}", excl=True) for i in range(NW)]
        def tile_gen(t):
            u_ = t % NW
            P.op("sp", lambda e: e.dma_start(out=xs[u_][:], in_=x_d[t * 128:(t + 1) * 128, :]), writes=[B_xs[u_]], lane=L_xs[u_])
            mix = PS[u_]
            for half in range(2):
                for kc in range(8):
                    P.op("pe", lambda e, kc=kc, half=half: e.matmul(mix[:, half * 512:(half + 1) * 512], lhsT=mgT[:, kc, t * 128:(t + 1) * 128], rhs=Wo[:, kc, half * 512:(half + 1) * 512],
                                                                     start=(kc == 0), stop=(kc == 7)), reads=[B_mgT[t // 4], B_Wo], writes=[B_mix[u_]])
            yield
            P.op("dve", lambda e: e.scalar_tensor_tensor(out=pre[u_][:], in0=mix[:, :], scalar=0.5 / ALPHA, in1=xs[u_][:], op0=ALU.mult, op1=ALU.add),
                 reads=[B_mix[u_], B_xs[u_]], writes=[B_pre[u_]])
            yield
            yield from ln.tile(pre[u_][:], [B_pre[u_]], acc[:, t, :], [B_acc[t]], lnp[:, 0, :], lnp[:, 1, :], B_lnp, LN_EPS / (ALPHA * ALPHA))
            P.op("act", lambda e: e.copy(out=x1b[u_][:], in_=acc[:, t, :]), reads=[B_acc[t]], writes=[B_x1b[u_]])
            yield
            pst = mix[:, 0:512].bitcast(BF16)
            for kc in range(8):
                P.op("pe", lambda e, kc=kc: e.transpose(pst[:, kc * 128:(kc + 1) * 128], x1b[u_][:, kc * 128:(kc + 1) * 128], ident_b[:]), reads=[B_x1b[u_], B_cstb], writes=[B_mix[u_]])
            yield
            P.op("dve", lambda e: e.tensor_copy(out=x1T[:, :, t * 128:(t + 1) * 128], in_=pst.rearrange("p (k t) -> p k t", k=8)), reads=[B_mix[u_]], writes=[B_x1T[t]])
            yield

        run_window([tile_gen(t) for t in range(NT)], NW, stagger=4)

    def phase_2():
        RA = Region(nc, BASE + 12 * KB, BASE + 76 * KB)
        RB = Region(nc, BASE + 172 * KB, TOP)
        wu = [RA.alloc(f"wu{i}", [128, 8, 512], BF16) for i in range(2)]
        wd = [RA.alloc(f"wd{i}", [128, 4, D], BF16) for i in range(2)]
        B_wu = [Buf(f"wu{i}") for i in range(2)]
        B_wd = [Buf(f"wd{i}") for i in range(2)]
        L_wu = [P.lane(f"wu{i}") for i in range(2)]
        L_wd = [P.lane(f"wd{i}") for i in range(2)]
        hr = [RA.alloc(f"hr{i}", [128, 512], F32) for i in range(2)]
        B_hr = [Buf(f"hr{i}") for i in range(2)]
        hT = [RA.alloc(f"hT{i}", [128, 4, 512], BF16) for i in range(2)]
        B_hT = [[Buf(f"hT{i}_{m}") for m in range(4)] for i in range(2)]
        lnp = RA.alloc("lnp2", [128, 2, D], F32)
        B_lnp = Buf("lnp2")
        llnp = P.lane("lnp2")
        P.op("sp", lambda e: e.dma_start(out=lnp[:].rearrange("p a d -> p (a d)"), in_=lnp_d[:, 2 * D:4 * D]), writes=[B_lnp], lane=llnp)
        ln = LN(RB, "ln2")
        ost = [RB.alloc(f"ost{i}", [128, D], F32) for i in range(2)]
        B_ost = [Buf(f"ost{i}") for i in range(2)]
        L_ost = [P.lane(f"ost{i}") for i in range(2)]
        B_hp = [Buf(f"hp{i}", excl=True) for i in range(4)]
        HP_BANK = (0, 1, 6, 7)
        B_y = [Buf("y0", excl=True), Buf("y1", excl=True)]
        w_up_v = w_up_d.rearrange("(kc p) c -> p kc c", p=128)
        w_dn_v = w_down_d.rearrange("(kc p) c -> p kc c", p=128)
        NF = 8
        pending = []
        ost4 = [RB.alloc(f"ostx{i}", [128, D], F32) for i in range(2)]
        ost_all = ost + ost4
        B_ost_all = B_ost + [Buf("ostx0"), Buf("ostx1")]
        L_ost_all = L_ost + [P.lane("ostx0"), P.lane("ostx1")]

        def ln2_chain(t):
            u_ = t % 4
            yield from ln.tile(acc[:, t, :], [B_acc[t]], ost_all[u_][:], [B_ost_all[u_]], lnp[:, 0, :], lnp[:, 1, :], B_lnp, LN_EPS / (ALPHA * ALPHA))
            P.op("sp", lambda e: e.dma_start(out=out_d[t * 128:(t + 1) * 128, :], in_=ost_all[u_][:]), reads=[B_ost_all[u_]], lane=L_ost_all[u_])
            yield

        def tick():
            for g_ in list(pending[:2]):
                try:
                    next(g_)
                except StopIteration:
                    pending.remove(g_)

        for f in range(NF):
            s = f % 2
            P.op("pool", lambda e, f=f, s=s: [e.dma_start(out=wu[s][:, 2 * j:2 * j + 2, :], in_=w_up_v[:, 2 * j:2 * j + 2, f * 512:(f + 1) * 512]) for j in range(4)],
                 writes=[B_wu[s]], lane=L_wu[s], ndma=4)
            P.op("pool", lambda e, f=f, s=s: [e.dma_start(out=wd[s][:, 2 * j:2 * j + 2, :], in_=w_dn_v[:, 4 * f + 2 * j:4 * f + 2 * j + 2, :]) for j in range(2)],
                 writes=[B_wd[s]], lane=L_wd[s], ndma=2)
            for tg in range(4):
                hs = tg % 2
                for mt in range(4):
                    hp, Bhp = bank(HP_BANK[mt % 4]), B_hp[mt % 4]
                    for kc in range(8):
                        P.op("pe", lambda e, kc=kc, mt=mt, hp=hp, s=s, tg=tg: e.matmul(hp, lhsT=wu[s][:, kc, mt * 128:(mt + 1) * 128], rhs=x1T[:, kc, tg * 512:(tg + 1) * 512], start=(kc == 0), stop=(kc == 7)),
                             reads=[B_wu[s]] + B_x1T[tg * 4:tg * 4 + 4], writes=[Bhp])
                    r_, Br = hr[mt % 2], B_hr[mt % 2]
                    P.op("act", lambda e, r_=r_, hp=hp: e.activation(out=r_[:], in_=hp, func=AF.Relu), reads=[Bhp], writes=[Br])
                    P.op("act", lambda e, r_=r_, hs=hs, mt=mt: e.activation(out=hT[hs][:, mt, :], in_=r_[:], func=AF.Square), reads=[Br], writes=[B_hT[hs][mt]])
                    tick()
                for tl in range(4):
                    t = tg * 4 + tl
                    y, By = PS[1 + tl % 2], B_y[tl % 2]
                    for half in range(2):
                        for mt in range(4):
                            P.op("pe", lambda e, mt=mt, half=half, y=y, hs=hs, tl=tl, s=s: e.matmul(y[:, half * 512:(half + 1) * 512], lhsT=hT[hs][:, mt, tl * 128:(tl + 1) * 128], rhs=wd[s][:, mt, half * 512:(half + 1) * 512],
                                                                                             start=(mt == 0), stop=(mt == 3)), reads=[B_hT[hs][mt], B_wd[s]], writes=[By])
                    P.op("dve", lambda e, t=t, y=y: e.scalar_tensor_tensor(out=acc[:, t, :], in0=y[:, :], scalar=1.0 / ALPHA, in1=acc[:, t, :], op0=ALU.mult, op1=ALU.add),
                         reads=[By, B_acc[t]], writes=[B_acc[t]])
                    if f == NF - 1:
                        pending.append(ln2_chain(t))
                    tick()

        while pending:
            tick()

    def dump_bf(name, src, bufs, shape):
        d = ddump(name, shape, BF16)
        l = P.lane("dbg_" + name)
        P.op("sp", lambda e: e.dma_start(out=d, in_=src), reads=bufs, lane=l)

    def dump_f(name, src, bufs, shape):
        d = ddump(name, shape, F32)
        l = P.lane("dbg_" + name)
        P.op("sp", lambda e: e.dma_start(out=d, in_=src), reads=bufs, lane=l)

    phases = [("1a", phase_1a), ("1b", phase_1b), ("1c", phase_1c), ("1d", phase_1d), ("2", phase_2)]
    for name, fn in phases:
        if stop_after == "0":
            break
        fn()
        P.barrier()
        if dbg:
            if name == "1a":
                dump_bf("d_obT", obT[:], B_obT, [128, 4, T_CORE])
            if name == "1b":
                dump_bf("d_oaT", oaT[:], B_oaT, [128, 4, T_CORE])
            if name == "1c":
                dump_bf("d_mgT", mgT[:], B_mgT, [128, 8, T_CORE])
            if name == "1d":
                dump_f("d_acc", acc[:], B_acc, [128, NT, D])
            P.barrier()
        if stop_after == name:
            break

    with nc.Block() as block:
        P.finalize(block)
    return nc, dbg_out


def _consts():
    p = np.arange(128)[:, None]
    f = np.arange(128)[None, :]
    ident = (p == f).astype(np.float32)
    U = (p <= f).astype(np.float32)
    ones = np.ones((128, 128), np.float32)
    MD = np.where(p <= f, 0.0, NEG).astype(np.float32)
    Mown = np.tile(np.where(p <= f, 0.0, NEG).astype(np.float32), (1, 4))
    Mprev = np.tile(np.where(p > f, 0.0, NEG).astype(np.float32), (1, 4))
    sel = np.zeros((128, 4, 128), np.float32)
    for h in range(4):
        sel[h, h, :] = 1.0
    MD4 = np.tile(np.where(p < f, 0.0, NEG).astype(np.float32), (1, 4))
    negsel = np.zeros((128, 4, 128), np.float32)
    blk = np.zeros((128, 4, 128), np.float32)
    for h in range(4):
        negsel[h, h, :] = -1.0
        blk[h, h, :] = 1.0
    cst = np.concatenate([ident, U, ones, MD, Mown, Mprev, sel.reshape(128, 512), MD4, negsel.reshape(128, 512), blk.reshape(128, 512)], axis=1)
    return np.ascontiguousarray(cst), Mprev


def make_in_maps(x, w_in, conv_w, attn_sinks, dn_a_log, dn_dt_bias, dn_norm_w, w_attn_out, w_dn_out, w_out,
                 ln1_g, ln1_b, w_up, w_down, ln2_g, ln2_b):
    f32 = np.float32
    x = np.asarray(x, f32)
    cst, Mprev = _consts()
    cw = np.ascontiguousarray(np.asarray(conv_w, f32)[0].reshape(4, 12, 128).transpose(2, 1, 0).reshape(128, 48))
    small = np.concatenate([np.asarray(attn_sinks, f32)[0], np.asarray(dn_a_log, f32)[0], np.asarray(dn_dt_bias, f32)[0]])
    small = np.ascontiguousarray(np.broadcast_to(small[None, :], (128, 16)))
    nw = np.ascontiguousarray(np.broadcast_to(np.asarray(dn_norm_w, f32)[0][None, :], (128, 128)))
    lnp = np.concatenate([np.asarray(a, f32)[0] for a in (ln1_g, ln1_b, ln2_g, ln2_b)])
    lnp = np.ascontiguousarray(np.broadcast_to(lnp[None, :], (128, 4 * D)))
    shared = {
        "cst": cst, "cw": cw, "small": small, "nw": nw, "lnp": lnp,
        "w_in": np.ascontiguousarray(np.asarray(w_in, f32)[0]),
        "w_ao": np.ascontiguousarray(np.asarray(w_attn_out, f32)[0]),
        "w_do": np.ascontiguousarray(np.asarray(w_dn_out, f32)[0]),
        "w_out": np.ascontiguousarray(np.asarray(w_out, f32)[0]),
        "w_up": np.ascontiguousarray(np.asarray(w_up, f32)[0]),
        "w_down": np.ascontiguousarray(np.asarray(w_down, f32)[0]),
    }
    zeros = np.zeros((T_CORE, D), f32)
    allneg = np.full((128, 512), NEG, f32)
    maps = []
    for c in range(8):
        b, half = c // 2, c % 2
        m = dict(shared)
        m["x"] = np.ascontiguousarray(x[b, half * T_CORE:(half + 1) * T_CORE])
        m["xp"] = np.ascontiguousarray(x[b, 0:T_CORE]) if half == 1 else zeros
        m["mprev0"] = Mprev if half == 1 else allneg
        maps.append(m)
    return maps


_NC_CACHE = {}


def kernel(**inputs):
    if "nc" not in _NC_CACHE:
        _NC_CACHE["nc"] = build()[0]
    nc = _NC_CACHE["nc"]
    maps = make_in_maps(**inputs)
    res = run_bass_kernel_spmd(nc, maps, core_ids=list(range(8)))
    out = np.empty((4, 4096, D), np.float32)
    for c in range(8):
        b, half = c // 2, c % 2
        out[b, half * T_CORE:(half + 1) * T_CORE] = res.results[c]["out"]
    return out
```
